# Optimizing a Trainium2 kernel written in Bass

```python
import jax, jax.numpy as jnp
from jax import lax
import numpy as np

D_MODEL = 1024
BATCH = 8
SEQ = 4096
DEPTH = 1

ATTN_GROUPS = ((128, 1), (512, 4), (2048, 16))
N_GROUPS = 3
ATTN_HEADS = 4
ATTN_HEAD_DIM = 128
ATTN_GROUP_WIDTH = ATTN_HEADS * ATTN_HEAD_DIM
ATTN_QKV_WIDTH = N_GROUPS * ATTN_GROUP_WIDTH
ATTN_OUT_WIDTH = ATTN_GROUP_WIDTH
RET_HEADS = 8
RET_KEY_DIM = 64
RET_VALUE_DIM = 128
RET_QK_WIDTH = RET_HEADS * RET_KEY_DIM
RET_V_WIDTH = RET_HEADS * RET_VALUE_DIM
RET_CHUNK = 128
ROPE_BASE = 10000.0
N_BRANCHES = 2
NORM_EPS = 1e-6
IN_SIZES = (ATTN_QKV_WIDTH, ATTN_QKV_WIDTH, ATTN_QKV_WIDTH, ATTN_OUT_WIDTH,
            RET_QK_WIDTH, RET_QK_WIDTH, RET_V_WIDTH, RET_V_WIDTH,
            N_BRANCHES * D_MODEL)
IN_WIDTH = sum(IN_SIZES)
SPLIT_POINTS = tuple(int(p) for p in np.cumsum(IN_SIZES)[:-1])

kernel_name = "hybrid_dilated_attn_retention_gated_block"


def rmsnorm(x, w):
    xf = x.astype(jnp.float32)
    y = xf * lax.rsqrt(jnp.mean(xf * xf, axis=-1, keepdims=True) + NORM_EPS)
    return (y * w.astype(jnp.float32)).astype(x.dtype)


def dilated_group_attention(q, k, v, window, dilation):
    B, S, H, E = q.shape
    d = dilation
    W = window // dilation
    L = S // d
    nb = -(-L // W)
    Lp = nb * W

    def to_sub(t):
        t = t.reshape(B, L, d, H, E).transpose(0, 2, 1, 3, 4)
        return jnp.pad(t, ((0, 0), (0, 0), (0, Lp - L), (0, 0), (0, 0)))

    def kv_blocks(t):
        t = jnp.pad(to_sub(t), ((0, 0), (0, 0), (W, 0), (0, 0), (0, 0)))
        t = t.reshape(B, d, nb + 1, W, H, E)
        return jnp.concatenate([t[:, :, :-1], t[:, :, 1:]], axis=3)

    def from_sub(t):
        rest = t.shape[4:]
        t = t.reshape((B, d, Lp) + rest)[:, :, :L]
        return jnp.swapaxes(t, 1, 2).reshape((B, S) + rest)

    qs = to_sub(q).reshape(B, d, nb, W, H, E)
    ks = kv_blocks(k)
    vs = kv_blocks(v)
    s = jnp.einsum('bdnqhe,bdnkhe->bdnhqk', qs, ks).astype(jnp.float32)
    a = jnp.arange(W)[None, :, None]
    c = jnp.arange(2 * W)[None, None, :]
    blk = jnp.arange(nb)[:, None, None]
    key_pos = (blk - 1) * W + c
    valid = (c >= a) & (c <= a + W) & (key_pos >= 0)
    s = jnp.where(valid[:, None], s, -jnp.inf)
    m = jnp.max(s, axis=-1, keepdims=True)
    p = jnp.exp(s - m)
    den = jnp.sum(p, axis=-1)
    o = jnp.einsum('bdnhqk,bdnkhe->bdnqhe', p, vs.astype(jnp.float32))
    den_t = jnp.swapaxes(den, -1, -2)
    o = o / den_t[..., None]
    lse = jnp.swapaxes(m[..., 0], -1, -2) + jnp.log(den_t)
    return from_sub(o), from_sub(lse)


def rotate_pairs(t, cos, sin):
    t1 = t[..., 0::2]
    t2 = t[..., 1::2]
    return jnp.stack([t1 * cos - t2 * sin, t1 * sin + t2 * cos], axis=-1).reshape(t.shape)


def retention(q, k, v, gn_w):
    B, S, H, dk = q.shape
    dv = v.shape[-1]
    q = q.astype(jnp.float32)
    k = k.astype(jnp.float32)
    v = v.astype(jnp.float32)
    pos = jnp.arange(S, dtype=jnp.float32)
    inv_freq = ROPE_BASE ** (-jnp.linspace(0.0, 1.0, dk // 2, dtype=jnp.float32))
    ang = pos[:, None] * inv_freq[None, :]
    cos = jnp.cos(ang)[:, None, :]
    sin = jnp.sin(ang)[:, None, :]
    q = rotate_pairs(q, cos, sin)
    k = rotate_pairs(k, cos, sin) * (dk ** -0.5)
    log_gamma = jnp.log(1.0 - 2.0 ** (-5.0 - jnp.arange(H, dtype=jnp.float32)))

    C = RET_CHUNK
    N = S // C
    qc = q.reshape(B, N, C, H, dk)
    kc = k.reshape(B, N, C, H, dk)
    vc = v.reshape(B, N, C, H, dv)
    idx = jnp.arange(C, dtype=jnp.float32)
    diff = idx[:, None] - idx[None, :]
    decay_intra = jnp.where(diff[None] >= 0,
                            jnp.exp(jnp.maximum(diff, 0.0)[None] * log_gamma[:, None, None]),
                            0.0)
    s = jnp.einsum('bnqhd,bnkhd->bnhqk', qc, kc) * decay_intra
    o_intra = jnp.einsum('bnhqk,bnkhe->bnqhe', s, vc)

    key_decay = jnp.exp((C - 1 - idx)[None, :] * log_gamma[:, None])
    kv = jnp.einsum('bnkhd,hk,bnkhe->bnhde', kc, key_decay, vc)
    chunk_decay = jnp.exp(C * log_gamma)[None, :, None, None]

    def step(R, kv_n):
        return chunk_decay * R + kv_n, R

    _, R_prev = lax.scan(step, jnp.zeros((B, H, dk, dv), jnp.float32),
                         jnp.moveaxis(kv, 1, 0))
    R_prev = jnp.moveaxis(R_prev, 0, 1)
    query_decay = jnp.exp((idx + 1.0)[None, :] * log_gamma[:, None])
    o_cross = jnp.einsum('bnqhd,bnhde,hq->bnqhe', qc, R_prev, query_decay)
    o = (o_intra + o_cross).reshape(B, S, H, dv)
    o = o * lax.rsqrt(jnp.mean(o * o, axis=-1, keepdims=True) + NORM_EPS)
    return o.reshape(B, S, H * dv) * gn_w.astype(jnp.float32)


def setup_inputs(seed: int = 0) -> dict:
    key = jax.random.key(seed)
    ks = jax.random.split(key, 10)
    f32 = jnp.float32
    x = jax.random.normal(ks[0], (BATCH, SEQ, D_MODEL), f32)
    ln1_w = 1.0 + 0.01 * jax.random.normal(ks[1], (DEPTH, D_MODEL), f32)
    w_in = jax.random.normal(ks[2], (DEPTH, D_MODEL, IN_WIDTH), f32) * D_MODEL ** -0.5
    b_gate = 0.01 * jax.random.normal(ks[3], (DEPTH, N_BRANCHES * D_MODEL), f32)
    attn_proj = jax.random.normal(ks[4], (DEPTH, ATTN_OUT_WIDTH, D_MODEL), f32) * ATTN_OUT_WIDTH ** -0.5
    ret_proj = jax.random.normal(ks[5], (DEPTH, RET_V_WIDTH, D_MODEL), f32) * RET_V_WIDTH ** -0.5
    ret_gn_w = 1.0 + 0.01 * jax.random.normal(ks[6], (DEPTH, RET_V_WIDTH), f32)
    w_out = jax.random.normal(ks[7], (DEPTH, D_MODEL, D_MODEL), f32) * D_MODEL ** -0.5
    lnf_w = 1.0 + 0.01 * jax.random.normal(ks[8], (D_MODEL,), f32)
    return {"x": x, "ln1_w": ln1_w, "w_in": w_in, "b_gate": b_gate,
            "attn_proj": attn_proj, "ret_proj": ret_proj, "ret_gn_w": ret_gn_w,
            "w_out": w_out, "lnf_w": lnf_w}


def reference(x, ln1_w, w_in, b_gate, attn_proj, ret_proj, ret_gn_w, w_out, lnf_w):
    B, S, _ = x.shape
    h = x
    for l in range(DEPTH):
        xn = rmsnorm(h, ln1_w[l])
        z = xn @ w_in[l]
        aq, ak, av, ag, rq, rk, rv, rg, mg = jnp.split(z, SPLIT_POINTS, axis=-1)

        aq = aq.reshape(B, S, N_GROUPS, ATTN_HEADS, ATTN_HEAD_DIM) * (ATTN_HEAD_DIM ** -0.5)
        ak = ak.reshape(B, S, N_GROUPS, ATTN_HEADS, ATTN_HEAD_DIM)
        av = av.reshape(B, S, N_GROUPS, ATTN_HEADS, ATTN_HEAD_DIM)
        outs = []
        lses = []
        for g, (win, dil) in enumerate(ATTN_GROUPS):
            o_g, lse_g = dilated_group_attention(aq[:, :, g], ak[:, :, g], av[:, :, g], win, dil)
            outs.append(o_g)
            lses.append(lse_g)
        mix = jax.nn.softmax(jnp.stack(lses, axis=0), axis=0)
        attn = jnp.einsum('gbsh,gbshe->bshe', mix, jnp.stack(outs, axis=0))
        attn = attn.reshape(B, S, ATTN_OUT_WIDTH).astype(x.dtype) * jax.nn.silu(ag)
        y_attn = attn @ attn_proj[l]

        ret = retention(rq.reshape(B, S, RET_HEADS, RET_KEY_DIM),
                        rk.reshape(B, S, RET_HEADS, RET_KEY_DIM),
                        rv.reshape(B, S, RET_HEADS, RET_VALUE_DIM),
                        ret_gn_w[l])
        y_ret = (ret.astype(x.dtype) * jax.nn.silu(rg)) @ ret_proj[l]

        gates = jax.nn.sigmoid(mg + b_gate[l]).reshape(B, S, N_BRANCHES, D_MODEL)
        merged = gates[:, :, 0] * y_attn + gates[:, :, 1] * y_ret
        h = h + merged @ w_out[l]
    return rmsnorm(h, lnf_w)
```

```python
import numpy as np
import concourse.bass as bass
import concourse.mybir as mybir
from concourse.bass_utils import run_bass_kernel_spmd

F32 = mybir.dt.float32
BF16 = mybir.dt.bfloat16
U8 = mybir.dt.uint8
AF = mybir.ActivationFunctionType
ALU = mybir.AluOpType

S = 4096
D = 1024
NCH = 32
INW = 10240
EPS = 1e-6
ATT_SCALE = float(128 ** -0.5)
GROUPS = ((128, 1), (512, 4), (2048, 16))
ARENA_BYTES = 212480


class Eng:
    def __init__(self, nc, eng, name):
        self.eng = eng
        self.name = name
        self.sem = nc.alloc_semaphore("pg_" + name)
        self.n = 0
        self.seen = {}

    def mark(self, ins):
        ins.then_inc(self.sem, 1)
        self.n += 1
        return (self, self.n)

    def wait(self, *deps):
        for dep in deps:
            if dep is None:
                continue
            if isinstance(dep, (list,)):
                self.wait(*dep)
                continue
            src, tick = dep
            if src is self:
                continue
            if self.seen.get(src.name, 0) >= tick:
                continue
            self.eng.wait_ge(src.sem, tick)
            self.seen[src.name] = tick


class DSem:
    def __init__(self, nc, name):
        self.name = name
        self.sem = nc.alloc_semaphore(name)
        self.n = 0

    def fire(self, ins):
        ins.then_inc(self.sem, 16)
        self.n += 16
        return (self, self.n)


class Ring:
    def __init__(self, items):
        self.items = items
        self.free = [[] for _ in items]
        self.ptr = 0

    def acquire(self, eng, skip_pending=False):
        i = self.ptr
        if skip_pending:
            for _ in range(len(self.items)):
                if self.free[i] is not None:
                    break
                i = (i + 1) % len(self.items)
        self.ptr = (i + 1) % len(self.items)
        assert self.free[i] is not None, "ring slot still pending"
        eng.wait(*self.free[i])
        self.free[i] = None
        return i

    def release(self, i, *deps):
        self.free[i] = [d for d in deps if d is not None]


class Arena:
    def __init__(self, nc, nbytes):
        self.t = nc.alloc_sbuf_tensor("arena", [128, nbytes], U8)
        self.off = 0
        self.cap = nbytes

    def alloc(self, free_shape, dtype):
        esz = 4 if dtype == F32 else 2
        n = esz
        for s_ in free_shape:
            n *= s_
        n_al = (n + 63) // 64 * 64
        assert self.off + n_al <= self.cap, f"arena overflow {self.off + n_al} > {self.cap}"
        ap = self.t[:, self.off:self.off + n].bitcast(dtype)
        self.off += n_al
        if len(free_shape) == 2:
            ap = ap.rearrange("p (a b) -> p a b", a=free_shape[0])
        elif len(free_shape) == 3:
            ap = ap.rearrange("p (a b c) -> p a b c", a=free_shape[0], b=free_shape[1])
        return ap


def host_consts():
    c = {}
    c["c_ident"] = np.eye(128, dtype=np.float32)
    kk = np.arange(128)[:, None]
    qq = np.arange(128)[None, :]
    mask = np.zeros((128, 2, 128), np.float32)
    mask[:, 0, :] = (kk <= qq)
    mask[:, 1, :] = (kk >= qq)
    c["c_mask"] = mask
    inv_freq = (10000.0 ** (-np.linspace(0.0, 1.0, 32, dtype=np.float32))).astype(np.float32)
    pos = np.arange(S, dtype=np.float32)
    ang = (pos[:, None] * inv_freq[None, :]).astype(np.float32)
    cos = np.cos(ang.astype(np.float64)).astype(np.float32).reshape(NCH, 128, 32).transpose(1, 0, 2)
    sin = np.sin(ang.astype(np.float64)).astype(np.float32).reshape(NCH, 128, 32).transpose(1, 0, 2)
    c["c_cos"] = np.ascontiguousarray(cos)
    c["c_sin"] = np.ascontiguousarray(sin)
    hh = np.arange(8, dtype=np.float64)
    lg = np.log(1.0 - 2.0 ** (-5.0 - hh))
    diff = (qq - kk).astype(np.float64)
    dec = np.where(diff[:, None, :] >= 0, np.exp(np.maximum(diff, 0.0)[:, None, :] * lg[None, :, None]), 0.0)
    c["c_decay"] = (0.125 * dec).astype(np.float32)
    a = np.arange(128, dtype=np.float64)
    qd = np.zeros((128, 4, 128), np.float64)
    for p in range(128):
        for j in range(4):
            h = 2 * j + p // 64
            qd[p, j, :] = np.exp((a + 1.0) * lg[h])
    c["c_qd"] = qd.astype(np.float32)
    kd = 0.125 * np.exp((127.0 - a)[:, None] * lg[None, :])
    c["c_kd"] = kd.astype(np.float32)
    gc = np.zeros((128, 4), np.float64)
    for p in range(128):
        for j in range(4):
            gc[p, j] = np.exp(128.0 * lg[2 * j + p // 64])
    c["c_gc"] = gc.astype(np.float32)
    return c


def build_nc(stage=2, dbg=None):
    nc = bass.Bass("TRN2", target_bir_lowering=False)

    def din(name, shape):
        return nc.dram_tensor(name, shape, F32, kind="ExternalInput").ap()

    x = din("x", [S, D])
    ln1_w = din("ln1_w", [D])
    w_in = din("w_in", [D, INW])
    b_gate = din("b_gate", [2048])
    attn_proj = din("attn_proj", [512, D])
    ret_proj = din("ret_proj", [D, D])
    ret_gn_w = din("ret_gn_w", [D])
    w_out = din("w_out", [D, D])
    lnf_w = din("lnf_w", [D])
    c_ident = din("c_ident", [128, 128])
    c_mask = din("c_mask", [128, 2, 128])
    c_cos = din("c_cos", [128, NCH, 32])
    c_sin = din("c_sin", [128, NCH, 32])
    c_decay = din("c_decay", [128, 8, 128])
    c_qd = din("c_qd", [128, 4, 128])
    c_kd = din("c_kd", [128, 8])
    c_gc = din("c_gc", [128, 4])
    out = nc.dram_tensor("out", [S, D], F32, kind="ExternalOutput").ap()
    attn_scr = nc.dram_tensor("attn_scr", [NCH, 128, 4, 128], BF16,
                              kind="ExternalOutput" if stage == 1 else "Internal").ap()

    w_in_v = w_in.rearrange("(k p) c -> p k c", p=128)

    PE = Eng(nc, nc.tensor, "pe")
    ACT = Eng(nc, nc.scalar, "act")
    DVE = Eng(nc, nc.vector, "dve")
    POOL = Eng(nc, nc.gpsimd, "pool")
    SP = Eng(nc, nc.sync, "sp")

    banks = [nc.alloc_psum_tensor(f"bank{i}", [128, 512], F32) for i in range(8)]
    psum = Ring(banks)

    def bk(b):
        return banks[b][:, :]

    def bk_bf(b):
        return banks[b][:, :].bitcast(BF16).rearrange("p (k t) -> p k t", k=8)

    def bk4(b):
        return banks[b][:, :].rearrange("p (k t) -> p k t", k=4)

    A = Arena(nc, ARENA_BYTES)

    ident_bf = A.alloc([128], BF16)
    ones_bf = A.alloc([128], BF16)
    mask_bf = A.alloc([2, 128], BF16)
    ln1_bc = A.alloc([D], F32)
    neghalf = A.alloc([8], F32)
    ss0 = A.alloc([NCH], F32)
    ms0 = A.alloc([NCH], F32)
    rstd0 = A.alloc([NCH], F32)

    csem = DSem(nc, "csem")
    csem2 = DSem(nc, "csem2")
    csem.fire(nc.gpsimd.dma_start(out=ident_bf, in_=c_ident[:, :]))
    d_c1 = csem.fire(nc.gpsimd.dma_start(out=mask_bf, in_=c_mask[:, :, :]))
    d_c2 = csem2.fire(nc.sync.dma_start(out=ln1_bc, in_=ln1_w.partition_broadcast(128)))
    nc.vector.memset(ones_bf, 1.0)
    nc.vector.memset(ss0, 0.0)
    t_init_dve = DVE.mark(nc.vector.memset(neghalf, -0.5))

    phase_mark = A.off
    wB_early = A.t[:, phase_mark:phase_mark + 8 * 5120 * 2].bitcast(BF16).rearrange("p (a b) -> p a b", a=8)
    wsems = [DSem(nc, f"wsemB{i}") for i in range(2)]
    wq = []

    def wdma(out_ap, in_ap):
        i = len(wq) % 2
        if len(wq) >= 2:
            POOL.wait(wq[-2])
        dep = wsems[i].fire(nc.gpsimd.dma_start(out=out_ap, in_=in_ap))
        wq.append(dep)
        return dep


    xs_bufs = [A.alloc([D], F32) for _ in range(4)]
    xs_ring = Ring(xs_bufs)
    xld = [DSem(nc, f"xld{i}") for i in range(4)]
    xnb_bufs = [A.alloc([D], BF16) for _ in range(2)]
    xnb_ring = Ring(xnb_bufs)
    qTb = [A.alloc([S], BF16) for _ in range(2)]
    kTb = [A.alloc([S], BF16) for _ in range(2)]
    vsbb = [A.alloc([32, 128], BF16) for _ in range(2)]
    vT = A.alloc([S], BF16)
    wbuf = [A.alloc([3, 8, 128], BF16) for _ in range(2)]
    z1_end = A.off
    assert z1_end - phase_mark >= 8 * 5120 * 2, "wB must fit in zone Z1"
    xnT = A.alloc([8, S], BF16)

    POOL.wait(t_init_dve)
    st1 = {}

    def p0_stage1(c):
        i = xs_ring.acquire(SP)
        d_ld = xld[i].fire(nc.sync.dma_start(out=xs_bufs[i], in_=x[c * 128:(c + 1) * 128, :]))
        j = xnb_ring.acquire(ACT)
        ACT.wait(d_ld)
        t_ss = ACT.mark(nc.scalar.activation(out=xnb_bufs[j], in_=xs_bufs[i], func=AF.Square,
                                             accum_out=ss0[:, c:c + 1]))
        DVE.wait(t_ss)
        t_ms = DVE.mark(nc.vector.tensor_scalar(out=ms0[:, c:c + 1], in0=ss0[:, c:c + 1],
                                                scalar1=1.0 / D, scalar2=EPS, op0=ALU.mult, op1=ALU.add))
        POOL.wait(t_ms)
        t_rs = POOL.mark(nc.gpsimd.tensor_tensor(out=rstd0[:, c:c + 1], in0=ms0[:, c:c + 1],
                                                 in1=neghalf[:, 0:1], op=ALU.pow))
        DVE.wait(t_rs, d_c2)
        t_xn = DVE.mark(nc.vector.scalar_tensor_tensor(out=xnb_bufs[j], in0=xs_bufs[i],
                                                       scalar=rstd0[:, c:c + 1], in1=ln1_bc,
                                                       op0=ALU.mult, op1=ALU.mult))
        xs_ring.release(i, t_xn)
        st1[c] = (j, t_xn)

    t_xnT = None
    xnT_ready = {}

    def p0_stage2(c):
        nonlocal t_xnT
        j, t_xn = st1.pop(c)
        b = psum.acquire(PE, True)
        PE.wait(t_xn, d_c1)
        for k in range(8):
            tr = nc.tensor.transpose(out=bk_bf(b)[:, k, :], in_=xnb_bufs[j][:, k * 128:(k + 1) * 128],
                                     identity=ident_bf)
        t_tr = PE.mark(tr)
        xnb_ring.release(j, t_tr)
        ACT.wait(t_tr)
        t_ev = ACT.mark(nc.scalar.copy(out=xnT[:, :, c * 128:(c + 1) * 128], in_=bk_bf(b)))
        psum.release(b, t_ev)
        t_xnT = t_ev
        xnT_ready[c] = t_ev

    def run_phase0(hook=None):
        for c in range(NCH + 1):
            if c < NCH:
                p0_stage1(c)
            if c >= 1:
                p0_stage2(c - 1)
                if hook is not None:
                    hook(c - 1)

    if stage == 0:
        run_phase0()
        dbg = nc.dram_tensor("dbg_xnT", [128, 8, S], BF16, kind="ExternalOutput").ap()
        SP.wait(t_xnT)
        dd = csem2.fire(nc.sync.dma_start(out=dbg[:, :, :], in_=xnT))
        SP.wait(dd)
        SP.wait((PE, PE.n), (ACT, ACT.n), (DVE, DVE.n), (POOL, POOL.n))
        return nc

    accOD = A.alloc([2, S], F32)
    w_ring = Ring(wbuf)
    wld = [DSem(nc, f"wld{i}") for i in range(2)]
    wag = [A.alloc([8, 128], BF16) for _ in range(2)]
    wag_ring = Ring(wag)
    wagld = [DSem(nc, f"wagld{i}") for i in range(2)]
    pt_bufs = [A.alloc([2, 128], BF16) for _ in range(3)]
    pt_ring = Ring(pt_bufs)
    tA_bufs = [A.alloc([512], F32) for _ in range(2)]
    tA_ring = Ring(tA_bufs)
    ao_bufs = [A.alloc([512], BF16) for _ in range(2)]
    ao_ring = Ring(ao_bufs)
    spill = [DSem(nc, f"spill{i}") for i in range(2)]
    phaseA_end = A.off

    groups = [(h, g) for h in range(4) for g in range(3)]
    wdeps = {}
    wagdeps = {}

    def wload(idx):
        h, g = groups[idx]
        sl = w_ring.acquire(POOL)
        for j in range(3):
            col = j * 1536 + g * 512 + h * 128
            dep = wld[sl].fire(nc.gpsimd.dma_start(out=wbuf[sl][:, j, :, :], in_=w_in_v[:, :, col:col + 128]))
        wdeps[idx] = (sl, dep)
        if g == 0:
            s2 = wag_ring.acquire(POOL)
            col = 4608 + h * 128
            dep2 = wagld[s2].fire(nc.gpsimd.dma_start(out=wag[s2], in_=w_in_v[:, :, col:col + 128]))
            wagdeps[h] = (s2, dep2)

    spill_deps = []
    gst = {}

    def proj_steps(idx):
        h, g = groups[idx]
        d = GROUPS[g][1]
        qT, kT, vsb = qTb[idx % 2], kTb[idx % 2], vsbb[idx % 2]
        st_ = gst[idx] = dict(t_evac=None, t_vT=None, t_vsb=None)
        dsts = {0: qT.rearrange("p (r i) -> p r i", r=d), 1: kT.rearrange("p (r i) -> p r i", r=d),
                2: vT.rearrange("p (r i) -> p r i", r=d)}
        steps = []

        def mk_proj(j, tb):
            def f():
                sl, wdep = wdeps[idx]
                PE.wait(wdep, xnT_ready[4 * tb + 3])
                b = psum.acquire(PE, True)
                for k in range(8):
                    mm = nc.tensor.matmul(bk(b), lhsT=wbuf[sl][:, j, k, :], rhs=xnT[:, k, tb * 512:(tb + 1) * 512],
                                          start=(k == 0), stop=(k == 7))
                t = PE.mark(mm)
                ACT.wait(t)
                dst = dsts[j][:, :, tb * 512 // d:(tb + 1) * 512 // d]
                src = bk(b).rearrange("p (i r) -> p r i", r=d)
                t_ev = ACT.mark(nc.scalar.copy(out=dst, in_=src))
                psum.release(b, t_ev)
                st_["t_evac"] = t_ev
                if j == 2:
                    st_["t_vT"] = t_ev
                if j == 1 and tb == 7:
                    w_ring.release(sl, t)
            return f

        def mk_tr(b4):
            def f():
                b = psum.acquire(PE, True)
                PE.wait(st_["t_vT"])
                for u in range(4):
                    ub = 4 * b4 + u
                    tr = nc.tensor.transpose(out=bk_bf(b)[:, u, :], in_=vT[:, 128 * ub:128 * (ub + 1)],
                                             identity=ident_bf)
                t = PE.mark(tr)
                DVE.wait(t)
                st_["t_vsb"] = DVE.mark(nc.vector.tensor_copy(out=vsb[:, 4 * b4:4 * b4 + 4, :],
                                                              in_=bk_bf(b)[:, 0:4, :]))
                psum.release(b, st_["t_vsb"])
                st_["t_trdone"] = t
            return f

        for j in (2, 0, 1):
            for tb in range(8):
                steps.append(mk_proj(j, tb))
        for b4 in range(8):
            steps.append(mk_tr(b4))
        return steps

    def unit_steps(idx):
        h, g = groups[idx]
        d = GROUPS[g][1]
        nb = (S // d) // 128
        qT, kT, vsb = qTb[idx % 2], kTb[idx % 2], vsbb[idx % 2]
        st_ = gst[idx]

        def emit_S(b):
            n = b % nb
            nk = 2 if n > 0 else 1
            bs = psum.acquire(PE, True)
            PE.wait(st_["t_evac"])
            mm = nc.tensor.matmul(bk4(bs)[:, 0, :], lhsT=kT[:, 128 * b:128 * (b + 1)],
                                  rhs=qT[:, 128 * b:128 * (b + 1)], start=True, stop=True)
            if n > 0:
                mm = nc.tensor.matmul(bk4(bs)[:, 1, :], lhsT=kT[:, 128 * (b - 1):128 * b],
                                      rhs=qT[:, 128 * b:128 * (b + 1)], start=True, stop=True)
            t_s = PE.mark(mm)
            p = pt_ring.acquire(ACT)
            ACT.wait(t_s)
            t_e = ACT.mark(nc.scalar.activation(out=pt_bufs[p][:, 0:nk, :], in_=bk4(bs)[:, 0:nk, :],
                                                func=AF.Exp, scale=ATT_SCALE))
            psum.release(bs, t_e)
            DVE.wait(t_e)
            t_m = DVE.mark(nc.vector.tensor_tensor(out=pt_bufs[p][:, 0:nk, :], in0=pt_bufs[p][:, 0:nk, :],
                                                   in1=mask_bf[:, 0:nk, :], op=ALU.mult))
            return p, t_m

        def emit_PV(b, p, t_m):
            n = b % nb
            r = b // nb
            tok0 = r + 128 * n * d
            bo = psum.acquire(PE, True)
            PE.wait(t_m, st_["t_vsb"])
            nc.tensor.matmul(bk4(bo)[:, 0, :], lhsT=vsb[:, b, :], rhs=pt_bufs[p][:, 0, :],
                             start=True, stop=(n == 0))
            if n > 0:
                nc.tensor.matmul(bk4(bo)[:, 0, :], lhsT=vsb[:, b - 1, :], rhs=pt_bufs[p][:, 1, :],
                                 start=False, stop=True)
            mm = nc.tensor.matmul(bk4(bo)[:, 1, :], lhsT=ones_bf, rhs=pt_bufs[p][:, 0, :],
                                  start=True, stop=(n == 0))
            if n > 0:
                mm = nc.tensor.matmul(bk4(bo)[:, 1, :], lhsT=ones_bf, rhs=pt_bufs[p][:, 1, :],
                                      start=False, stop=True)
            t_pv = PE.mark(mm)
            pt_ring.release(p, t_pv)
            DVE.wait(t_pv)
            dst = accOD[:, :, tok0:tok0 + 127 * d + 1:d]
            if g == 0:
                ins = nc.vector.tensor_copy(out=dst, in_=bk4(bo)[:, 0:2, :])
            else:
                ins = nc.vector.tensor_tensor(out=dst, in0=bk4(bo)[:, 0:2, :], in1=dst, op=ALU.add)
            t_acc = DVE.mark(ins)
            psum.release(bo, t_acc)

        pend = {}
        steps = []

        def mk(b):
            def f():
                if b < 32:
                    pend[b] = emit_S(b)
                if b >= 1:
                    emit_PV(b - 1, *pend.pop(b - 1))
            return f

        for b in range(33):
            steps.append(mk(b))
        return steps

    def finalize_steps(h):
        def mk(tb):
            def f():
                s2, dep2 = wagdeps[h]
                PE.wait(dep2)
                blk = slice(tb * 512, (tb + 1) * 512)
                b = psum.acquire(PE, True)
                for k in range(8):
                    mm = nc.tensor.matmul(bk(b), lhsT=wag[s2][:, k, :], rhs=xnT[:, k, blk],
                                          start=(k == 0), stop=(k == 7))
                t = PE.mark(mm)
                ta = tA_ring.acquire(ACT)
                ACT.wait(t)
                t1 = ACT.mark(nc.scalar.activation(out=tA_bufs[ta], in_=bk(b), func=AF.Tanh, scale=0.5))
                DVE.wait(t1)
                t_u = DVE.mark(nc.vector.scalar_tensor_tensor(out=tA_bufs[ta], in0=tA_bufs[ta], scalar=1.0,
                                                              in1=bk(b), op0=ALU.add, op1=ALU.mult))
                psum.release(b, t_u)
                nc.vector.reciprocal(out=accOD[:, 1, blk], in_=accOD[:, 1, blk])
                nc.vector.tensor_tensor(out=accOD[:, 0, blk], in0=accOD[:, 0, blk], in1=accOD[:, 1, blk],
                                        op=ALU.mult)
                ai = ao_ring.acquire(DVE)
                t_o = DVE.mark(nc.vector.scalar_tensor_tensor(out=ao_bufs[ai], in0=accOD[:, 0, blk], scalar=0.5,
                                                              in1=tA_bufs[ta], op0=ALU.mult, op1=ALU.mult))
                tA_ring.release(ta, t_o)
                SP.wait(t_o)
                dd = spill[ai].fire(nc.sync.dma_start(
                    out=attn_scr[4 * tb:4 * tb + 4, :, h, :].rearrange("c p t -> p c t"),
                    in_=ao_bufs[ai].rearrange("p (c t) -> p c t", c=4)))
                ao_ring.release(ai, dd)
                spill_deps.append(dd)
                if tb == 7:
                    wag_ring.release(s2, t)
            return f

        return [mk(tb) for tb in range(8)]

    def run_slot(lists):
        lists = [l for l in lists if l]
        if not lists:
            return
        n = max(len(l) for l in lists)
        pos = [0] * len(lists)
        for i in range(n):
            for li, l in enumerate(lists):
                tgt = (i + 1) * len(l) // n
                while pos[li] < tgt:
                    l[pos[li]]()
                    pos[li] += 1

    wload(0)
    wload(1)
    p0steps = proj_steps(0)

    def p0_hook(c):
        if c % 4 == 3:
            tb = c // 4
            for jj in range(3):
                p0steps[jj * 8 + tb]()

    run_phase0(p0_hook)
    for f_ in p0steps[24:]:
        f_()
    wload_next = [2]

    def after_proj():
        if wload_next[0] < 12:
            wload(wload_next[0])
            wload_next[0] += 1

    for h in range(4):
        g0, g1, g2 = 3 * h, 3 * h + 1, 3 * h + 2
        if h == 0:
            run_slot([unit_steps(g0), proj_steps(g1)])
            after_proj()
        else:
            run_slot([unit_steps(g0)])
        run_slot([unit_steps(g1), proj_steps(g2)])
        after_proj()
        if h < 3:
            run_slot([unit_steps(g2), proj_steps(g2 + 1)])
            after_proj()
            run_slot([finalize_steps(h), proj_steps(g2 + 2)])
            after_proj()
        else:
            run_slot([unit_steps(g2)])
            POOL.wait((PE, PE.n), (ACT, ACT.n), (DVE, DVE.n))
            for k in range(8):
                wdma(wB_early[:, k, :], w_in_v[:, k, 5120:10240])
            run_slot([finalize_steps(h)])

    if stage == 1:
        SP.wait(*spill_deps[-2:])
        SP.wait((PE, PE.n), (ACT, ACT.n), (DVE, DVE.n), (POOL, POOL.n))
        return nc


    bar = [(PE, PE.n), (ACT, ACT.n), (DVE, DVE.n), (POOL, POOL.n)] + spill_deps[-2:]
    for e_ in (POOL, SP, PE, ACT, DVE):
        e_.wait(*bar)
    A.off = phase_mark

    wB = A.alloc([8, 5120], BF16)
    wrp = A.alloc([8, D], BF16)
    wout_sb = A.alloc([8, D], BF16)
    wap = A.alloc([4, D], BF16)
    cosT = A.alloc([NCH, 32], F32)
    sinT = A.alloc([NCH, 32], F32)
    decay = A.alloc([8, 128], F32)
    qd = A.alloc([4, 128], F32)
    kd = A.alloc([8], F32)
    gcol = A.alloc([4], F32)
    gnw_col = A.alloc([8], F32)
    lnf_bc = A.alloc([D], F32)
    bg_bf = A.alloc([2048], BF16)
    R2 = A.alloc([4, 128], F32)
    R_bf = A.alloc([4, 128], BF16)
    ssB = A.alloc([NCH], F32)
    msB = A.alloc([NCH], F32)
    rstdB = A.alloc([NCH], F32)
    ssF = A.alloc([NCH], F32)
    msF = A.alloc([NCH], F32)
    rstdF = A.alloc([NCH], F32)
    ssG = A.alloc([NCH * 8], F32)
    msG = A.alloc([NCH * 8], F32)
    rsG = A.alloc([NCH * 8], F32)
    nhalf8 = A.alloc([8], F32)
    xc_bufs = [A.alloc([D], F32) for _ in range(3)]
    xc_ring = Ring(xc_bufs)
    xcld = [DSem(nc, f"xcld{i}") for i in range(3)]
    ost = [DSem(nc, f"ost{i}") for i in range(3)]
    xnbB = A.alloc([D], BF16)
    xnTc = A.alloc([8, 128], BF16)
    rtA = A.alloc([8, 32], F32)
    rtB = A.alloc([8, 32], F32)
    qrot = A.alloc([512], BF16)
    krot = A.alloc([512], BF16)
    kdec = A.alloc([512], BF16)
    qkT = A.alloc([8, 128], BF16)
    qdT = A.alloc([4, 128], BF16)
    sTm = A.alloc([8, 128], BF16)
    vbf = A.alloc([D], BF16)
    tAu = A.alloc([D], F32)
    tBn = A.alloc([512], F32)
    retg = A.alloc([D], BF16)
    retgT = A.alloc([8, 128], BF16)
    tG = A.alloc([2048], F32)
    merged = A.alloc([D], BF16)
    mT = A.alloc([8, 128], BF16)
    attc_bufs = [A.alloc([4, 128], BF16) for _ in range(2)]
    attc_ring = Ring(attc_bufs)
    attld = [DSem(nc, f"attld{i}") for i in range(2)]
    junkb = A.alloc([128], BF16)

    wsem = DSem(nc, "wsem")
    csem3 = DSem(nc, "csem3")
    nchB = dbg.get('nchB', NCH) if dbg else NCH
    xload = {}

    def issue_xload(c):
        i = xc_ring.acquire(SP)
        dep = xcld[i].fire(nc.sync.dma_start(out=xc_bufs[i], in_=x[c * 128:(c + 1) * 128, :]))
        xload[c] = (i, dep)

    attload = {}

    def issue_attload(c):
        a = attc_ring.acquire(SP)
        dep2 = attld[a].fire(nc.sync.dma_start(out=attc_bufs[a], in_=attn_scr[c, :, :, :]))
        attload[c] = (a, dep2)

    sgi = xc_ring.acquire(SP)
    if nchB > 0:
        issue_xload(0)

    assert len(wq) == 8
    d_wB = wq[6:8]
    d_wap = [wdma(wap, attn_proj.rearrange("(k p) c -> p k c", p=128)), wq[-2]]
    d_bg = [wdma(bg_bf[0:1, :], b_gate[None, :]), wq[-2]]
    d_wout = [wdma(wout_sb, w_out.rearrange("(k p) c -> p k c", p=128)), wq[-2]]
    for src_, dst_ in ((c_cos, cosT), (c_sin, sinT), (c_decay, decay), (c_qd, qd), (c_kd, kd), (c_gc, gcol)):
        csem3.fire(nc.sync.dma_start(out=dst_, in_=src_))
    csem3.fire(nc.sync.dma_start(out=gnw_col, in_=ret_gn_w.rearrange("(k p) -> p k", p=128),
                                 allow_slow_non_contiguous=True))
    d_c3 = csem3.fire(nc.sync.dma_start(out=lnf_bc, in_=lnf_w.partition_broadcast(128)))
    rp_v = ret_proj.rearrange("(k p) c -> p k c", p=128)
    nc.vector.memset(R2, 0.0)
    nc.vector.memset(R_bf, 0.0)
    nc.vector.memset(ssB, 0.0)
    nc.vector.memset(ssF, 0.0)
    nc.vector.memset(ssG, 0.0)
    t_initB = DVE.mark(nc.vector.memset(nhalf8, -0.5))
    DVE.wait(d_c3)
    POOL.wait(t_initB)

    def rstd_chain(ss_ap, ms_ap, rs_ap, t_in, inv_n, neg_ap):
        DVE.wait(t_in)
        t_ms = DVE.mark(nc.vector.tensor_scalar(out=ms_ap, in0=ss_ap, scalar1=inv_n, scalar2=EPS,
                                                op0=ALU.mult, op1=ALU.add))
        POOL.wait(t_ms)
        return POOL.mark(nc.gpsimd.tensor_tensor(out=rs_ap, in0=ms_ap, in1=neg_ap, op=ALU.pow))

    free = {}

    def fr(name):
        return free.get(name)

    st = {}

    def P_norm(c):
        xi, d_x = xload.pop(c)
        s_ = st[c] = dict(xi=xi, xc=xc_bufs[xi])
        xc = s_["xc"]
        ACT.wait(d_x, fr("xnbB"))
        t_ss = ACT.mark(nc.scalar.activation(out=xnbB, in_=xc, func=AF.Square, accum_out=ssB[:, c:c + 1]))
        t_rs = rstd_chain(ssB[:, c:c + 1], msB[:, c:c + 1], rstdB[:, c:c + 1], t_ss, 1.0 / D, nhalf8[:, 0:1])
        DVE.wait(t_rs)
        s_["t_xn"] = DVE.mark(nc.vector.scalar_tensor_tensor(out=xnbB, in0=xc, scalar=rstdB[:, c:c + 1],
                                                             in1=ln1_bc, op0=ALU.mult, op1=ALU.mult))

    def P_T1(c):
        s_ = st[c]
        b0 = psum.acquire(PE, True)
        PE.wait(s_["t_xn"])
        for k in range(8):
            tr = nc.tensor.transpose(out=bk_bf(b0)[:, k, :], in_=xnbB[:, k * 128:(k + 1) * 128], identity=ident_bf)
        t_T1 = PE.mark(tr)
        free["xnbB"] = t_T1
        ACT.wait(t_T1, fr("xnTc"))
        s_["t_xT"] = ACT.mark(nc.scalar.copy(out=xnTc, in_=bk_bf(b0)))
        psum.release(b0, s_["t_xT"])

    def proj(c, col0):
        b = psum.acquire(PE, True)
        PE.wait(st[c]["t_xT"], d_wB)
        for k in range(8):
            mm = nc.tensor.matmul(bk(b), lhsT=xnTc[:, k, :], rhs=wB[:, k, col0:col0 + 512],
                                  start=(k == 0), stop=(k == 7))
        return b, mm

    def rope(c, b, dst, t_in, extra=None):
        cosb = cosT[:, c, :].unsqueeze(1).to_broadcast([128, 8, 32])
        sinb = sinT[:, c, :].unsqueeze(1).to_broadcast([128, 8, 32])
        src = bk(b).rearrange("p (h i two) -> p h i two", h=8, two=2)
        dv = dst.rearrange("p (h i two) -> p h i two", h=8, two=2)
        t1 = src[:, :, :, 0]
        t2 = src[:, :, :, 1]
        DVE.wait(t_in, extra)
        nc.vector.tensor_tensor(out=rtA, in0=t1, in1=cosb, op=ALU.mult)
        nc.vector.tensor_tensor(out=rtB, in0=t2, in1=sinb, op=ALU.mult)
        nc.vector.tensor_tensor(out=dv[:, :, :, 0], in0=rtA, in1=rtB, op=ALU.subtract)
        nc.vector.tensor_tensor(out=rtA, in0=t1, in1=sinb, op=ALU.mult)
        nc.vector.tensor_tensor(out=rtB, in0=t2, in1=cosb, op=ALU.mult)
        return DVE.mark(nc.vector.tensor_tensor(out=dv[:, :, :, 1], in0=rtA, in1=rtB, op=ALU.add))

    def P_projqk_pe(c):
        s_ = st[c]
        bq, mm = proj(c, 0)
        t_pq = PE.mark(mm)
        bkk, mm = proj(c, 512)
        t_pk = PE.mark(mm)
        s_["pqk"] = (bq, t_pq, bkk, t_pk)

    def P_rope(c):
        s_ = st[c]
        bq, t_pq, bkk, t_pk = s_["pqk"]
        s_["t_rq"] = rope(c, bq, qrot, t_pq, fr("qkrot"))
        psum.release(bq, s_["t_rq"])
        s_["t_rk"] = rope(c, bkk, krot, t_pk)
        psum.release(bkk, s_["t_rk"])
        DVE.wait(fr("kdec"))
        s_["t_kd"] = DVE.mark(nc.vector.tensor_tensor(
            out=kdec.rearrange("p (h e) -> p h e", h=8), in0=krot.rearrange("p (h e) -> p h e", h=8),
            in1=kd[:, :].unsqueeze(2).to_broadcast([128, 8, 64]), op=ALU.mult))

    def P_projv(c):
        s_ = st[c]
        bv0, mm = proj(c, 1024)
        bv1, mm = proj(c, 1536)
        t_pv = PE.mark(mm)
        ACT.wait(t_pv, fr("vbf"))
        nc.scalar.copy(out=vbf[:, 0:512], in_=bk(bv0))
        s_["t_v"] = ACT.mark(nc.scalar.copy(out=vbf[:, 512:1024], in_=bk(bv1)))
        psum.release(bv0, s_["t_v"])
        psum.release(bv1, s_["t_v"])

    def P_projrg(c):
        s_ = st[c]
        brg0, mm = proj(c, 2048)
        brg1, mm = proj(c, 2560)
        t_prg = PE.mark(mm)
        ACT.wait(t_prg, fr("tAu"))
        nc.scalar.activation(out=tAu[:, 0:512], in_=bk(brg0), func=AF.Tanh, scale=0.5)
        t_tg = ACT.mark(nc.scalar.activation(out=tAu[:, 512:1024], in_=bk(brg1), func=AF.Tanh, scale=0.5))
        DVE.wait(t_tg)
        nc.vector.scalar_tensor_tensor(out=tAu[:, 0:512], in0=tAu[:, 0:512], scalar=1.0, in1=bk(brg0),
                                       op0=ALU.add, op1=ALU.mult)
        s_["t_u"] = DVE.mark(nc.vector.scalar_tensor_tensor(out=tAu[:, 512:1024], in0=tAu[:, 512:1024], scalar=1.0,
                                                            in1=bk(brg1), op0=ALU.add, op1=ALU.mult))
        psum.release(brg0, s_["t_u"])
        psum.release(brg1, s_["t_u"])

    def P_T2(c):
        s_ = st[c]
        bT2 = psum.acquire(PE, True)
        PE.wait(s_["t_rq"], s_["t_rk"])
        for j in range(4):
            nc.tensor.transpose(out=bk_bf(bT2)[:, j, :], in_=qrot[:, j * 128:(j + 1) * 128], identity=ident_bf)
        for j in range(4):
            tr = nc.tensor.transpose(out=bk_bf(bT2)[:, 4 + j, :], in_=krot[:, j * 128:(j + 1) * 128],
                                     identity=ident_bf)
        t_T2 = PE.mark(tr)
        free["qkrot"] = t_T2
        ACT.wait(t_T2, fr("qkT"))
        s_["t_qk"] = ACT.mark(nc.scalar.copy(out=qkT, in_=bk_bf(bT2)))
        DVE.wait(t_T2, fr("qdT"))
        s_["t_qd"] = DVE.mark(nc.vector.tensor_tensor(out=qdT, in0=bk_bf(bT2)[:, 0:4, :], in1=qd, op=ALU.mult))
        psum.release(bT2, s_["t_qk"], s_["t_qd"])

    def P_S(c):
        s_ = st[c]
        for pb in range(2):
            b = psum.acquire(PE, True)
            PE.wait(s_["t_qk"])
            po = 64 * pb
            for j in range(4):
                mm = nc.tensor.matmul(bk4(b)[:, j, :], lhsT=qkT[po:po + 64, 4 + j, :], rhs=qkT[po:po + 64, j, :],
                                      start=True, stop=True)
            t_S = PE.mark(mm)
            DVE.wait(t_S, fr("sTm"))
            s_["t_sm"] = DVE.mark(nc.vector.tensor_tensor(out=sTm[:, pb::2, :], in0=bk4(b),
                                                          in1=decay[:, pb::2, :], op=ALU.mult))
            psum.release(b, s_["t_sm"])

    def P_o(c):
        s_ = st[c]
        bo = []
        for pb in range(2):
            b = psum.acquire(PE, True)
            PE.wait(s_["t_sm"], s_["t_v"], s_["t_qd"], fr("R_bf_ready"))
            po = 64 * pb
            for j in range(4):
                h = 2 * j + pb
                nc.tensor.matmul(bk4(b)[:, j, :], lhsT=sTm[:, h, :], rhs=vbf[:, h * 128:(h + 1) * 128],
                                 start=True, stop=False)
                mm = nc.tensor.matmul(bk4(b)[:, j, :], lhsT=qdT[po:po + 64, j, :], rhs=R_bf[po:po + 64, j, :],
                                      start=False, stop=True)
            t_o = PE.mark(mm)
            bo.append(b)
        s_["bo"] = bo
        s_["t_o"] = t_o
        free["sTm"] = t_o
        free["qdT"] = t_o
        free["qkT"] = t_o

    def P_kv(c):
        s_ = st[c]
        bkv = []
        for jb in range(2):
            b = psum.acquire(PE, True)
            PE.wait(s_["t_kd"], s_["t_v"])
            for jj in range(2):
                j = 2 * jb + jj
                mm = nc.tensor.matmul(bk(b)[:, jj * 256:(jj + 1) * 256], lhsT=kdec[:, j * 128:(j + 1) * 128],
                                      rhs=vbf[:, j * 256:(j + 1) * 256], start=True, stop=True)
            bkv.append(b)
        t_kv = PE.mark(mm)
        free["kdec"] = t_kv
        free["vbf"] = t_kv
        DVE.wait(t_kv, s_["t_o"])
        nc.vector.tensor_tensor(out=R2, in0=R2, in1=gcol[:, :].unsqueeze(2).to_broadcast([128, 4, 128]),
                                op=ALU.mult)
        for jb in range(2):
            kvv = bk(bkv[jb]).rearrange("p (j e) -> p j e", j=2)
            nc.vector.tensor_tensor(out=R2[0:64, 2 * jb:2 * jb + 2, :], in0=kvv[0:64, :, 0:128],
                                    in1=R2[0:64, 2 * jb:2 * jb + 2, :], op=ALU.add)
            t_R = DVE.mark(nc.vector.tensor_tensor(out=R2[64:128, 2 * jb:2 * jb + 2, :], in0=kvv[64:128, :, 128:256],
                                                   in1=R2[64:128, 2 * jb:2 * jb + 2, :], op=ALU.add))
            psum.release(bkv[jb], t_R)
        free["R_bf_ready"] = DVE.mark(nc.vector.tensor_copy(out=R_bf, in_=R2))

    def P_mg(c, jqs):
        s_ = st[c]
        for jq in jqs:
            b = psum.acquire(PE, True)
            PE.wait(s_["t_xT"], d_wB, d_bg)
            for k in range(8):
                nc.tensor.matmul(bk(b), lhsT=xnTc[:, k, :], rhs=wB[:, k, 3072 + jq * 512:3072 + (jq + 1) * 512],
                                 start=(k == 0), stop=False)
            mm = nc.tensor.matmul(bk(b), lhsT=ones_bf[0:1, :], rhs=bg_bf[0:1, jq * 512:(jq + 1) * 512],
                                  start=False, stop=True)
            t_mg = PE.mark(mm)
            if jq == 3:
                free["xnTc"] = t_mg
            if jq == 0:
                ACT.wait(fr("tG"))
            ACT.wait(t_mg)
            s_["t_tG"] = ACT.mark(nc.scalar.activation(out=tG[:, jq * 512:(jq + 1) * 512], in_=bk(b),
                                                       func=AF.Tanh, scale=0.5))
            psum.release(b, s_["t_tG"])

    def P_gn(c):
        s_ = st[c]
        bo = s_["bo"]
        ACT.wait(s_["t_o"])
        for h in range(8):
            t_sq = ACT.mark(nc.scalar.activation(out=junkb, in_=bk4(bo[h % 2])[:, h // 2, :], func=AF.Square,
                                                 accum_out=ssG[:, c * 8 + h:c * 8 + h + 1]))
        t_rg = rstd_chain(ssG[:, c * 8:c * 8 + 8], msG[:, c * 8:c * 8 + 8], rsG[:, c * 8:c * 8 + 8], t_sq,
                          1.0 / 128, nhalf8)
        DVE.wait(t_rg, s_["t_u"], fr("retg"))
        retg_v = retg.rearrange("p (h e) -> p h e", h=8)
        tAu_v = tAu.rearrange("p (h e) -> p h e", h=8)
        tBn_v = tBn.rearrange("p (h e) -> p h e", h=4)
        for pb in range(2):
            nc.vector.tensor_tensor(out=tBn_v, in0=bk4(bo[pb]),
                                    in1=rsG[:, c * 8 + pb:c * 8 + 8:2].unsqueeze(2).to_broadcast([128, 4, 128]),
                                    op=ALU.mult)
            t_ret = DVE.mark(nc.vector.scalar_tensor_tensor(out=retg_v[:, pb::2, :], in0=tBn_v, scalar=0.5,
                                                            in1=tAu_v[:, pb::2, :],
                                                            op0=ALU.mult, op1=ALU.mult))
            psum.release(bo[pb], t_ret, t_sq)
        s_["t_ret"] = t_ret
        free["tAu"] = t_ret

    def P_T3(c):
        s_ = st[c]
        bT3 = psum.acquire(PE, True)
        PE.wait(s_["t_ret"])
        for k in range(8):
            tr = nc.tensor.transpose(out=bk_bf(bT3)[:, k, :], in_=retg[:, k * 128:(k + 1) * 128], identity=ident_bf)
        t_T3 = PE.mark(tr)
        free["retg"] = t_T3
        ACT.wait(t_T3, fr("retgT"))
        s_["t_rT"] = ACT.mark(nc.scalar.copy(out=retgT, in_=bk_bf(bT3)))
        psum.release(bT3, s_["t_rT"])

    def P_yattn(c):
        s_ = st[c]
        ai, d_att = attload.pop(c)
        bya = []
        for jj in range(2):
            b = psum.acquire(PE, True)
            PE.wait(d_att, d_wap)
            for k in range(4):
                mm = nc.tensor.matmul(bk(b), lhsT=attc_bufs[ai][:, k, :], rhs=wap[:, k, jj * 512:(jj + 1) * 512],
                                      start=(k == 0), stop=(k == 3))
            bya.append(b)
        s_["t_ya"] = PE.mark(mm)
        s_["bya"] = bya
        attc_ring.release(ai, s_["t_ya"])
        if c + 2 < nchB:
            issue_attload(c + 2)

    def P_yret(c):
        s_ = st[c]
        byr = []
        for jj in range(2):
            b = psum.acquire(PE, True)
            PE.wait(s_["t_rT"], wrp_state["t"])
            for k in range(8):
                mm = nc.tensor.matmul(bk(b), lhsT=retgT[:, k, :], rhs=wrp[:, k, jj * 512:(jj + 1) * 512],
                                      start=(k == 0), stop=(k == 7))
            byr.append(b)
        s_["t_yr"] = PE.mark(mm)
        s_["byr"] = byr
        free["retgT"] = s_["t_yr"]

    def P_merge(c):
        s_ = st[c]
        bya, byr = s_["bya"], s_["byr"]
        DVE.wait(s_["t_ya"], s_["t_yr"], s_["t_tG"], fr("merged"))
        for jj in range(2):
            s0 = slice(jj * 512, (jj + 1) * 512)
            s1 = slice(1024 + jj * 512, 1024 + (jj + 1) * 512)
            nc.vector.scalar_tensor_tensor(out=tG[:, s0], in0=tG[:, s0], scalar=1.0, in1=bk(bya[jj]),
                                           op0=ALU.add, op1=ALU.mult)
            nc.vector.scalar_tensor_tensor(out=tG[:, s1], in0=tG[:, s1], scalar=1.0, in1=bk(byr[jj]),
                                           op0=ALU.add, op1=ALU.mult)
            t_mg2 = DVE.mark(nc.vector.tensor_tensor(out=merged[:, s0], in0=tG[:, s0], in1=tG[:, s1], op=ALU.add))
            psum.release(bya[jj], t_mg2)
            psum.release(byr[jj], t_mg2)
        s_["t_mg2"] = t_mg2
        free["tG"] = t_mg2

    def P_T4(c):
        s_ = st[c]
        bT4 = psum.acquire(PE, True)
        PE.wait(s_["t_mg2"])
        for k in range(8):
            tr = nc.tensor.transpose(out=bk_bf(bT4)[:, k, :], in_=merged[:, k * 128:(k + 1) * 128], identity=ident_bf)
        t_T4 = PE.mark(tr)
        s_["t_T4"] = t_T4
        ACT.wait(t_T4, fr("mT"))
        s_["t_mT"] = ACT.mark(nc.scalar.copy(out=mT, in_=bk_bf(bT4)))
        psum.release(bT4, s_["t_mT"])

    def P_out(c):
        s_ = st[c]
        xc = s_["xc"]
        xi = s_["xi"]
        for jj in range(2):
            b = psum.acquire(PE, True)
            PE.wait(s_["t_mT"], d_wout)
            for k in range(8):
                mm = nc.tensor.matmul(bk(b), lhsT=mT[:, k, :], rhs=wout_sb[:, k, jj * 512:(jj + 1) * 512],
                                      start=(k == 0), stop=(k == 7))
            t_po = PE.mark(mm)
            DVE.wait(t_po)
            sl_ = slice(jj * 512, (jj + 1) * 512)
            t_h = DVE.mark(nc.vector.scalar_tensor_tensor(out=xc[:, sl_], in0=bk(b), scalar=0.5, in1=xc[:, sl_],
                                                          op0=ALU.mult, op1=ALU.add))
            psum.release(b, t_h)
        free["mT"] = t_po
        ACT.wait(t_h)
        t_ssf = ACT.mark(nc.scalar.activation(out=merged, in_=xc, func=AF.Square, accum_out=ssF[:, c:c + 1]))
        free["merged"] = [s_["t_T4"], t_ssf]
        t_rsf = rstd_chain(ssF[:, c:c + 1], msF[:, c:c + 1], rstdF[:, c:c + 1], t_ssf, 1.0 / D, nhalf8[:, 0:1])
        DVE.wait(t_rsf)
        t_out = DVE.mark(nc.vector.scalar_tensor_tensor(out=xc, in0=xc, scalar=rstdF[:, c:c + 1], in1=lnf_bc,
                                                        op0=ALU.mult, op1=ALU.mult))
        SP.wait(t_out)
        d_o = ost[xi].fire(nc.sync.dma_start(out=out[c * 128:(c + 1) * 128, :], in_=xc))
        xc_ring.release(xi, d_o)
        del st[c]
        return d_o

    NPF = 2
    for c in range(min(NPF, nchB)):
        if c > 0:
            issue_xload(c)
        issue_attload(c)
    stg_sem = DSem(nc, "stg")
    wrp_state = {"t": None}

    def mk_wrp(k):
        def f():
            SP.wait(wrp_state["t"])
            dd = stg_sem.fire(nc.sync.dma_start(out=xc_bufs[sgi], in_=rp_v[:, k, :]))
            DVE.wait(dd, d_c3)
            wrp_state["t"] = DVE.mark(nc.vector.tensor_scalar(out=wrp[:, k, :], in0=xc_bufs[sgi],
                                                              scalar1=gnw_col[:, k:k + 1], scalar2=None,
                                                              op0=ALU.mult))
            if k == 7:
                xc_ring.release(sgi, wrp_state["t"])
        return f

    wrp_steps = [mk_wrp(k) for k in range(8)]

    def wrp_step():
        if wrp_steps:
            wrp_steps.pop(0)()
    d_last = []
    if nchB > 0:
        P_norm(0)
        wrp_step()
        P_T1(0)
        wrp_step()
        P_projqk_pe(0)
        P_rope(0)
        wrp_step()
        P_projv(0)
        wrp_step()
        P_projrg(0)
        wrp_step()
    for c in range(nchB):
        nxt = c + 1 < nchB
        if nxt:
            P_norm(c + 1)
        P_T2(c)
        wrp_step()
        P_yattn(c)
        wrp_step()
        P_S(c)
        wrp_step()
        if c + NPF < nchB:
            issue_xload(c + NPF)
        P_mg(c, [0])
        P_o(c)
        P_kv(c)
        P_gn(c)
        P_mg(c, [1, 2, 3])
        P_T3(c)
        if nxt:
            P_T1(c + 1)
        P_yret(c)
        if nxt:
            P_projqk_pe(c + 1)
        P_merge(c)
        if nxt:
            P_rope(c + 1)
        P_T4(c)
        if nxt:
            P_projv(c + 1)
        d_last.append(P_out(c))
        if nxt:
            P_projrg(c + 1)
    SP.wait(*[d_ for d_ in d_last[-3:] if d_ is not None])
    SP.wait((PE, PE.n), (ACT, ACT.n), (DVE, DVE.n), (POOL, POOL.n))
    return nc


def kernel(x, ln1_w, w_in, b_gate, attn_proj, ret_proj, ret_gn_w, w_out, lnf_w, _stage=2, _cores=8, _dbg=None):
    nc = build_nc(stage=_stage, dbg=_dbg)
    consts = host_consts()
    x = np.ascontiguousarray(x, dtype=np.float32)
    shared = {
        "ln1_w": np.ascontiguousarray(ln1_w[0], dtype=np.float32),
        "w_in": np.ascontiguousarray(w_in[0], dtype=np.float32),
        "b_gate": np.ascontiguousarray(b_gate[0], dtype=np.float32),
        "attn_proj": np.ascontiguousarray(attn_proj[0], dtype=np.float32),
        "ret_proj": np.ascontiguousarray(ret_proj[0], dtype=np.float32),
        "ret_gn_w": np.ascontiguousarray(ret_gn_w[0], dtype=np.float32),
        "w_out": np.ascontiguousarray(w_out[0], dtype=np.float32),
        "lnf_w": np.ascontiguousarray(lnf_w, dtype=np.float32),
    }
    shared.update(consts)
    in_maps = [dict(shared, x=x[i]) for i in range(_cores)]
    res = run_bass_kernel_spmd(nc, in_maps, core_ids=list(range(_cores)))
    if _dbg is not None:
        return res.results
    if _stage == 0:
        return [r["dbg_xnT"] for r in res.results]
    if _stage == 1:
        return [r["attn_scr"] for r in res.results]
    return np.stack([r["out"] for r in res.results], axis=0)
```

```python
import numpy as np
import concourse.bass as bass
import concourse.mybir as mybir
from concourse.bass_utils import run_bass_kernel_spmd

F32 = mybir.dt.float32
BF16 = mybir.dt.bfloat16
U8 = mybir.dt.uint8
AF = mybir.ActivationFunctionType
ALU = mybir.AluOpType

S = 4096
D = 1024
NCH = 32
INW = 10240
EPS = 1e-6
ATT_SCALE = float(128 ** -0.5)
GROUPS = ((128, 1), (512, 4), (2048, 16))
ARENA_BYTES = 212480


class Eng:
    def __init__(self, nc, eng, name):
        self.eng = eng
        self.name = name
        self.sem = nc.alloc_semaphore("pg_" + name)
        self.n = 0
        self.seen = {}

    def mark(self, ins):
        ins.then_inc(self.sem, 1)
        self.n += 1
        return (self, self.n)

    def wait(self, *deps):
        for dep in deps:
            if dep is None:
                continue
            if isinstance(dep, (list,)):
                self.wait(*dep)
                continue
            src, tick = dep
            if src is self:
                continue
            if self.seen.get(src.name, 0) >= tick:
                continue
            self.eng.wait_ge(src.sem, tick)
            self.seen[src.name] = tick


class DSem:
    def __init__(self, nc, name):
        self.name = name
        self.sem = nc.alloc_semaphore(name)
        self.n = 0

    def fire(self, ins):
        ins.then_inc(self.sem, 16)
        self.n += 16
        return (self, self.n)


class Ring:
    def __init__(self, items):
        self.items = items
        self.free = [[] for _ in items]
        self.ptr = 0

    def acquire(self, eng, skip_pending=False):
        i = self.ptr
        if skip_pending:
            for _ in range(len(self.items)):
                if self.free[i] is not None:
                    break
                i = (i + 1) % len(self.items)
        self.ptr = (i + 1) % len(self.items)
        assert self.free[i] is not None, "ring slot still pending"
        eng.wait(*self.free[i])
        self.free[i] = None
        return i

    def release(self, i, *deps):
        self.free[i] = [d for d in deps if d is not None]


class Arena:
    def __init__(self, nc, nbytes):
        self.t = nc.alloc_sbuf_tensor("arena", [128, nbytes], U8)
        self.off = 0
        self.cap = nbytes

    def alloc(self, free_shape, dtype):
        esz = 4 if dtype == F32 else 2
        n = esz
        for s_ in free_shape:
            n *= s_
        n_al = (n + 63) // 64 * 64
        assert self.off + n_al <= self.cap, f"arena overflow {self.off + n_al} > {self.cap}"
        ap = self.t[:, self.off:self.off + n].bitcast(dtype)
        self.off += n_al
        if len(free_shape) == 2:
            ap = ap.rearrange("p (a b) -> p a b", a=free_shape[0])
        elif len(free_shape) == 3:
            ap = ap.rearrange("p (a b c) -> p a b c", a=free_shape[0], b=free_shape[1])
        return ap


def host_consts():
    c = {}
    c["c_ident"] = np.eye(128, dtype=np.float32)
    kk = np.arange(128)[:, None]
    qq = np.arange(128)[None, :]
    mask = np.zeros((128, 2, 128), np.float32)
    mask[:, 0, :] = (kk <= qq)
    mask[:, 1, :] = (kk >= qq)
    c["c_mask"] = mask
    inv_freq = (10000.0 ** (-np.linspace(0.0, 1.0, 32, dtype=np.float32))).astype(np.float32)
    pos = np.arange(S, dtype=np.float32)
    ang = (pos[:, None] * inv_freq[None, :]).astype(np.float32)
    cos = np.cos(ang.astype(np.float64)).astype(np.float32).reshape(NCH, 128, 32).transpose(1, 0, 2)
    sin = np.sin(ang.astype(np.float64)).astype(np.float32).reshape(NCH, 128, 32).transpose(1, 0, 2)
    c["c_cos"] = np.ascontiguousarray(cos)
    c["c_sin"] = np.ascontiguousarray(sin)
    hh = np.arange(8, dtype=np.float64)
    lg = np.log(1.0 - 2.0 ** (-5.0 - hh))
    diff = (qq - kk).astype(np.float64)
    dec = np.where(diff[:, None, :] >= 0, np.exp(np.maximum(diff, 0.0)[:, None, :] * lg[None, :, None]), 0.0)
    c["c_decay"] = (0.125 * dec).astype(np.float32)
    a = np.arange(128, dtype=np.float64)
    qd = np.zeros((128, 4, 128), np.float64)
    for p in range(128):
        for j in range(4):
            h = 2 * j + p // 64
            qd[p, j, :] = np.exp((a + 1.0) * lg[h])
    c["c_qd"] = qd.astype(np.float32)
    kd = 0.125 * np.exp((127.0 - a)[:, None] * lg[None, :])
    c["c_kd"] = kd.astype(np.float32)
    gc = np.zeros((128, 4), np.float64)
    for p in range(128):
        for j in range(4):
            gc[p, j] = np.exp(128.0 * lg[2 * j + p // 64])
    c["c_gc"] = gc.astype(np.float32)
    return c


def build_nc(stage=2, dbg=None):
    nc = bass.Bass("TRN2", target_bir_lowering=False)

    def din(name, shape):
        return nc.dram_tensor(name, shape, F32, kind="ExternalInput").ap()

    x = din("x", [S, D])
    ln1_w = din("ln1_w", [D])
    w_in = din("w_in", [D, INW])
    b_gate = din("b_gate", [2048])
    attn_proj = din("attn_proj", [512, D])
    ret_proj = din("ret_proj", [D, D])
    ret_gn_w = din("ret_gn_w", [D])
    w_out = din("w_out", [D, D])
    lnf_w = din("lnf_w", [D])
    c_ident = din("c_ident", [128, 128])
    c_mask = din("c_mask", [128, 2, 128])
    c_cos = din("c_cos", [128, NCH, 32])
    c_sin = din("c_sin", [128, NCH, 32])
    c_decay = din("c_decay", [128, 8, 128])
    c_qd = din("c_qd", [128, 4, 128])
    c_kd = din("c_kd", [128, 8])
    c_gc = din("c_gc", [128, 4])
    out = nc.dram_tensor("out", [S, D], F32, kind="ExternalOutput").ap()
    attn_scr = nc.dram_tensor("attn_scr", [NCH, 128, 4, 128], BF16,
                              kind="ExternalOutput" if stage == 1 else "Internal").ap()

    w_in_v = w_in.rearrange("(k p) c -> p k c", p=128)

    PE = Eng(nc, nc.tensor, "pe")
    ACT = Eng(nc, nc.scalar, "act")
    DVE = Eng(nc, nc.vector, "dve")
    POOL = Eng(nc, nc.gpsimd, "pool")
    SP = Eng(nc, nc.sync, "sp")

    banks = [nc.alloc_psum_tensor(f"bank{i}", [128, 512], F32) for i in range(8)]
    psum = Ring(banks)

    def bk(b):
        return banks[b][:, :]

    def bk_bf(b):
        return banks[b][:, :].bitcast(BF16).rearrange("p (k t) -> p k t", k=8)

    def bk4(b):
        return banks[b][:, :].rearrange("p (k t) -> p k t", k=4)

    A = Arena(nc, ARENA_BYTES)

    ident_bf = A.alloc([128], BF16)
    ones_bf = A.alloc([128], BF16)
    mask_bf = A.alloc([2, 128], BF16)
    ln1_bc = A.alloc([D], F32)
    neghalf = A.alloc([8], F32)
    ss0 = A.alloc([NCH], F32)
    ms0 = A.alloc([NCH], F32)
    rstd0 = A.alloc([NCH], F32)

    csem = DSem(nc, "csem")
    csem2 = DSem(nc, "csem2")
    csem.fire(nc.gpsimd.dma_start(out=ident_bf, in_=c_ident[:, :]))
    d_c1 = csem.fire(nc.gpsimd.dma_start(out=mask_bf, in_=c_mask[:, :, :]))
    d_c2 = csem2.fire(nc.sync.dma_start(out=ln1_bc, in_=ln1_w.partition_broadcast(128)))
    nc.vector.memset(ones_bf, 1.0)
    nc.vector.memset(ss0, 0.0)
    t_init_dve = DVE.mark(nc.vector.memset(neghalf, -0.5))

    phase_mark = A.off
    wB_early = A.t[:, phase_mark:phase_mark + 8 * 5120 * 2].bitcast(BF16).rearrange("p (a b) -> p a b", a=8)
    wsems = [DSem(nc, f"wsemB{i}") for i in range(2)]
    wq = []

    def wdma(out_ap, in_ap):
        i = len(wq) % 2
        if len(wq) >= 2:
            POOL.wait(wq[-2])
        dep = wsems[i].fire(nc.gpsimd.dma_start(out=out_ap, in_=in_ap))
        wq.append(dep)
        return dep


    xs_bufs = [A.alloc([D], F32) for _ in range(4)]
    xs_ring = Ring(xs_bufs)
    xld = [DSem(nc, f"xld{i}") for i in range(4)]
    xnb_bufs = [A.alloc([D], BF16) for _ in range(2)]
    xnb_ring = Ring(xnb_bufs)
    qTb = [A.alloc([S], BF16) for _ in range(2)]
    kTb = [A.alloc([S], BF16) for _ in range(2)]
    vsbb = [A.alloc([32, 128], BF16) for _ in range(2)]
    vT = A.alloc([S], BF16)
    wbuf = [A.alloc([3, 8, 128], BF16) for _ in range(2)]
    z1_end = A.off
    assert z1_end - phase_mark >= 8 * 5120 * 2, "wB must fit in zone Z1"
    xnT = A.alloc([8, S], BF16)

    POOL.wait(t_init_dve)
    st1 = {}

    def p0_stage1(c):
        i = xs_ring.acquire(SP)
        d_ld = xld[i].fire(nc.sync.dma_start(out=xs_bufs[i], in_=x[c * 128:(c + 1) * 128, :]))
        j = xnb_ring.acquire(ACT)
        ACT.wait(d_ld)
        t_ss = ACT.mark(nc.scalar.activation(out=xnb_bufs[j], in_=xs_bufs[i], func=AF.Square,
                                             accum_out=ss0[:, c:c + 1]))
        DVE.wait(t_ss)
        t_ms = DVE.mark(nc.vector.tensor_scalar(out=ms0[:, c:c + 1], in0=ss0[:, c:c + 1],
                                                scalar1=1.0 / D, scalar2=EPS, op0=ALU.mult, op1=ALU.add))
        POOL.wait(t_ms)
        t_rs = POOL.mark(nc.gpsimd.tensor_tensor(out=rstd0[:, c:c + 1], in0=ms0[:, c:c + 1],
                                                 in1=neghalf[:, 0:1], op=ALU.pow))
        DVE.wait(t_rs, d_c2)
        t_xn = DVE.mark(nc.vector.scalar_tensor_tensor(out=xnb_bufs[j], in0=xs_bufs[i],
                                                       scalar=rstd0[:, c:c + 1], in1=ln1_bc,
                                                       op0=ALU.mult, op1=ALU.mult))
        xs_ring.release(i, t_xn)
        st1[c] = (j, t_xn)

    t_xnT = None
    xnT_ready = {}

    def p0_stage2(c):
        nonlocal t_xnT
        j, t_xn = st1.pop(c)
        b = psum.acquire(PE, True)
        PE.wait(t_xn, d_c1)
        for k in range(8):
            tr = nc.tensor.transpose(out=bk_bf(b)[:, k, :], in_=xnb_bufs[j][:, k * 128:(k + 1) * 128],
                                     identity=ident_bf)
        t_tr = PE.mark(tr)
        xnb_ring.release(j, t_tr)
        ACT.wait(t_tr)
        t_ev = ACT.mark(nc.scalar.copy(out=xnT[:, :, c * 128:(c + 1) * 128], in_=bk_bf(b)))
        psum.release(b, t_ev)
        t_xnT = t_ev
        xnT_ready[c] = t_ev

    def run_phase0(hook=None):
        for c in range(NCH + 1):
            if c < NCH:
                p0_stage1(c)
            if c >= 1:
                p0_stage2(c - 1)
                if hook is not None:
                    hook(c - 1)

    if stage == 0:
        run_phase0()
        dbg = nc.dram_tensor("dbg_xnT", [128, 8, S], BF16, kind="ExternalOutput").ap()
        SP.wait(t_xnT)
        dd = csem2.fire(nc.sync.dma_start(out=dbg[:, :, :], in_=xnT))
        SP.wait(dd)
        SP.wait((PE, PE.n), (ACT, ACT.n), (DVE, DVE.n), (POOL, POOL.n))
        return nc

    accOD = A.alloc([2, S], F32)
    w_ring = Ring(wbuf)
    wld = [DSem(nc, f"wld{i}") for i in range(2)]
    wag = [A.alloc([8, 128], BF16) for _ in range(2)]
    wag_ring = Ring(wag)
    wagld = [DSem(nc, f"wagld{i}") for i in range(2)]
    pt_bufs = [A.alloc([2, 128], BF16) for _ in range(3)]
    pt_ring = Ring(pt_bufs)
    tA_bufs = [A.alloc([512], F32) for _ in range(2)]
    tA_ring = Ring(tA_bufs)
    ao_bufs = [A.alloc([512], BF16) for _ in range(2)]
    ao_ring = Ring(ao_bufs)
    spill = [DSem(nc, f"spill{i}") for i in range(2)]
    phaseA_end = A.off

    groups = [(h, g) for h in range(4) for g in range(3)]
    wdeps = {}
    wagdeps = {}

    def wload(idx):
        h, g = groups[idx]
        sl = w_ring.acquire(POOL)
        for j in range(3):
            col = j * 1536 + g * 512 + h * 128
            dep = wld[sl].fire(nc.gpsimd.dma_start(out=wbuf[sl][:, j, :, :], in_=w_in_v[:, :, col:col + 128]))
        wdeps[idx] = (sl, dep)
        if g == 0:
            s2 = wag_ring.acquire(POOL)
            col = 4608 + h * 128
            dep2 = wagld[s2].fire(nc.gpsimd.dma_start(out=wag[s2], in_=w_in_v[:, :, col:col + 128]))
            wagdeps[h] = (s2, dep2)

    spill_deps = []
    gst = {}

    def proj_steps(idx):
        h, g = groups[idx]
        d = GROUPS[g][1]
        qT, kT, vsb = qTb[idx % 2], kTb[idx % 2], vsbb[idx % 2]
        st_ = gst[idx] = dict(t_evac=None, t_vT=None, t_vsb=None)
        dsts = {0: qT.rearrange("p (r i) -> p r i", r=d), 1: kT.rearrange("p (r i) -> p r i", r=d),
                2: vT.rearrange("p (r i) -> p r i", r=d)}
        steps = []

        def mk_proj(j, tb):
            def f():
                sl, wdep = wdeps[idx]
                PE.wait(wdep, xnT_ready[4 * tb + 3])
                b = psum.acquire(PE, True)
                for k in range(8):
                    mm = nc.tensor.matmul(bk(b), lhsT=wbuf[sl][:, j, k, :], rhs=xnT[:, k, tb * 512:(tb + 1) * 512],
                                          start=(k == 0), stop=(k == 7))
                t = PE.mark(mm)
                ACT.wait(t)
                dst = dsts[j][:, :, tb * 512 // d:(tb + 1) * 512 // d]
                src = bk(b).rearrange("p (i r) -> p r i", r=d)
                t_ev = ACT.mark(nc.scalar.copy(out=dst, in_=src))
                psum.release(b, t_ev)
                st_["t_evac"] = t_ev
                if j == 2:
                    st_["t_vT"] = t_ev
                if j == 1 and tb == 7:
                    w_ring.release(sl, t)
            return f

        def mk_tr(b4):
            def f():
                b = psum.acquire(PE, True)
                PE.wait(st_["t_vT"])
                for u in range(4):
                    ub = 4 * b4 + u
                    tr = nc.tensor.transpose(out=bk_bf(b)[:, u, :], in_=vT[:, 128 * ub:128 * (ub + 1)],
                                             identity=ident_bf)
                t = PE.mark(tr)
                DVE.wait(t)
                st_["t_vsb"] = DVE.mark(nc.vector.tensor_copy(out=vsb[:, 4 * b4:4 * b4 + 4, :],
                                                              in_=bk_bf(b)[:, 0:4, :]))
                psum.release(b, st_["t_vsb"])
                st_["t_trdone"] = t
            return f

        for j in (2, 0, 1):
            for tb in range(8):
                steps.append(mk_proj(j, tb))
        for b4 in range(8):
            steps.append(mk_tr(b4))
        return steps

    def unit_steps(idx):
        h, g = groups[idx]
        d = GROUPS[g][1]
        nb = (S // d) // 128
        qT, kT, vsb = qTb[idx % 2], kTb[idx % 2], vsbb[idx % 2]
        st_ = gst[idx]

        def emit_S(b):
            n = b % nb
            nk = 2 if n > 0 else 1
            bs = psum.acquire(PE, True)
            PE.wait(st_["t_evac"])
            mm = nc.tensor.matmul(bk4(bs)[:, 0, :], lhsT=kT[:, 128 * b:128 * (b + 1)],
                                  rhs=qT[:, 128 * b:128 * (b + 1)], start=True, stop=True)
            if n > 0:
                mm = nc.tensor.matmul(bk4(bs)[:, 1, :], lhsT=kT[:, 128 * (b - 1):128 * b],
                                      rhs=qT[:, 128 * b:128 * (b + 1)], start=True, stop=True)
            t_s = PE.mark(mm)
            p = pt_ring.acquire(ACT)
            ACT.wait(t_s)
            t_e = ACT.mark(nc.scalar.activation(out=pt_bufs[p][:, 0:nk, :], in_=bk4(bs)[:, 0:nk, :],
                                                func=AF.Exp, scale=ATT_SCALE))
            psum.release(bs, t_e)
            DVE.wait(t_e)
            t_m = DVE.mark(nc.vector.tensor_tensor(out=pt_bufs[p][:, 0:nk, :], in0=pt_bufs[p][:, 0:nk, :],
                                                   in1=mask_bf[:, 0:nk, :], op=ALU.mult))
            return p, t_m

        def emit_PV(b, p, t_m):
            n = b % nb
            r = b // nb
            tok0 = r + 128 * n * d
            bo = psum.acquire(PE, True)
            PE.wait(t_m, st_["t_vsb"])
            nc.tensor.matmul(bk4(bo)[:, 0, :], lhsT=vsb[:, b, :], rhs=pt_bufs[p][:, 0, :],
                             start=True, stop=(n == 0))
            if n > 0:
                nc.tensor.matmul(bk4(bo)[:, 0, :], lhsT=vsb[:, b - 1, :], rhs=pt_bufs[p][:, 1, :],
                                 start=False, stop=True)
            mm = nc.tensor.matmul(bk4(bo)[:, 1, :], lhsT=ones_bf, rhs=pt_bufs[p][:, 0, :],
                                  start=True, stop=(n == 0))
            if n > 0:
                mm = nc.tensor.matmul(bk4(bo)[:, 1, :], lhsT=ones_bf, rhs=pt_bufs[p][:, 1, :],
                                      start=False, stop=True)
            t_pv = PE.mark(mm)
            pt_ring.release(p, t_pv)
            DVE.wait(t_pv)
            dst = accOD[:, :, tok0:tok0 + 127 * d + 1:d]
            if g == 0:
                ins = nc.vector.tensor_copy(out=dst, in_=bk4(bo)[:, 0:2, :])
            else:
                ins = nc.vector.tensor_tensor(out=dst, in0=bk4(bo)[:, 0:2, :], in1=dst, op=ALU.add)
            t_acc = DVE.mark(ins)
            psum.release(bo, t_acc)

        pend = {}
        steps = []

        def mk(b):
            def f():
                if b < 32:
                    pend[b] = emit_S(b)
                if b >= 1:
                    emit_PV(b - 1, *pend.pop(b - 1))
            return f

        for b in range(33):
            steps.append(mk(b))
        return steps

    def finalize_steps(h):
        def mk(tb):
            def f():
                s2, dep2 = wagdeps[h]
                PE.wait(dep2)
                blk = slice(tb * 512, (tb + 1) * 512)
                b = psum.acquire(PE, True)
                for k in range(8):
                    mm = nc.tensor.matmul(bk(b), lhsT=wag[s2][:, k, :], rhs=xnT[:, k, blk],
                                          start=(k == 0), stop=(k == 7))
                t = PE.mark(mm)
                ta = tA_ring.acquire(ACT)
                ACT.wait(t)
                t1 = ACT.mark(nc.scalar.activation(out=tA_bufs[ta], in_=bk(b), func=AF.Tanh, scale=0.5))
                DVE.wait(t1)
                t_u = DVE.mark(nc.vector.scalar_tensor_tensor(out=tA_bufs[ta], in0=tA_bufs[ta], scalar=1.0,
                                                              in1=bk(b), op0=ALU.add, op1=ALU.mult))
                psum.release(b, t_u)
                nc.vector.reciprocal(out=accOD[:, 1, blk], in_=accOD[:, 1, blk])
                nc.vector.tensor_tensor(out=accOD[:, 0, blk], in0=accOD[:, 0, blk], in1=accOD[:, 1, blk],
                                        op=ALU.mult)
                ai = ao_ring.acquire(DVE)
                t_o = DVE.mark(nc.vector.scalar_tensor_tensor(out=ao_bufs[ai], in0=accOD[:, 0, blk], scalar=0.5,
                                                              in1=tA_bufs[ta], op0=ALU.mult, op1=ALU.mult))
                tA_ring.release(ta, t_o)
                SP.wait(t_o)
                dd = spill[ai].fire(nc.sync.dma_start(
                    out=attn_scr[4 * tb:4 * tb + 4, :, h, :].rearrange("c p t -> p c t"),
                    in_=ao_bufs[ai].rearrange("p (c t) -> p c t", c=4)))
                ao_ring.release(ai, dd)
                spill_deps.append(dd)
                if tb == 7:
                    wag_ring.release(s2, t)
            return f

        return [mk(tb) for tb in range(8)]

    def run_slot(lists):
        lists = [l for l in lists if l]
        if not lists:
            return
        n = max(len(l) for l in lists)
        pos = [0] * len(lists)
        for i in range(n):
            for li, l in enumerate(lists):
                tgt = (i + 1) * len(l) // n
                while pos[li] < tgt:
                    l[pos[li]]()
                    pos[li] += 1

    wload(0)
    wload(1)
    p0steps = proj_steps(0)

    p0_pending = []

    def p0_hook(c):
        if c % 4 == 3:
            tb = c // 4
            for jj in range(3):
                p0_pending.append(p0steps[jj * 8 + tb])
        if p0_pending:
            p0_pending.pop(0)()

    run_phase0(p0_hook)
    for f_ in p0_pending + p0steps[24:]:
        f_()
    wload_next = [2]

    def after_proj():
        if wload_next[0] < 12:
            wload(wload_next[0])
            wload_next[0] += 1

    for h in range(4):
        g0, g1, g2 = 3 * h, 3 * h + 1, 3 * h + 2
        if h == 0:
            run_slot([unit_steps(g0), proj_steps(g1)])
            after_proj()
        else:
            run_slot([unit_steps(g0)])
        run_slot([unit_steps(g1), proj_steps(g2)])
        after_proj()
        if h < 3:
            run_slot([unit_steps(g2), proj_steps(g2 + 1)])
            after_proj()
            run_slot([finalize_steps(h), proj_steps(g2 + 2)])
            after_proj()
        else:
            run_slot([unit_steps(g2)])
            POOL.wait((PE, PE.n), (ACT, ACT.n), (DVE, DVE.n))
            for k in range(8):
                wdma(wB_early[:, k, :], w_in_v[:, k, 5120:10240])
            run_slot([finalize_steps(h)])

    if stage == 1:
        SP.wait(*spill_deps[-2:])
        SP.wait((PE, PE.n), (ACT, ACT.n), (DVE, DVE.n), (POOL, POOL.n))
        return nc


    bar = [(PE, PE.n), (ACT, ACT.n), (DVE, DVE.n), (POOL, POOL.n)] + spill_deps[-2:]
    for e_ in (POOL, SP, PE, ACT, DVE):
        e_.wait(*bar)
    A.off = phase_mark

    wB = A.alloc([8, 5120], BF16)
    wrp = A.alloc([8, D], BF16)
    wout_sb = A.alloc([8, D], BF16)
    wap = A.alloc([4, D], BF16)
    cosT = A.alloc([NCH, 32], F32)
    sinT = A.alloc([NCH, 32], F32)
    decay = A.alloc([8, 128], F32)
    qd = A.alloc([4, 128], F32)
    kd = A.alloc([8], F32)
    gcol = A.alloc([4], F32)
    gnw_col = A.alloc([8], F32)
    lnf_bc = A.alloc([D], F32)
    bg_bf = A.alloc([2048], BF16)
    R2 = A.alloc([4, 128], F32)
    R_bf = A.alloc([4, 128], BF16)
    ssB = A.alloc([NCH], F32)
    msB = A.alloc([NCH], F32)
    rstdB = A.alloc([NCH], F32)
    ssF = A.alloc([NCH], F32)
    msF = A.alloc([NCH], F32)
    rstdF = A.alloc([NCH], F32)
    ssG = A.alloc([NCH * 8], F32)
    msG = A.alloc([NCH * 8], F32)
    rsG = A.alloc([NCH * 8], F32)
    nhalf8 = A.alloc([8], F32)
    xc_bufs = [A.alloc([D], F32) for _ in range(3)]
    xc_ring = Ring(xc_bufs)
    xcld = [DSem(nc, f"xcld{i}") for i in range(3)]
    ost = [DSem(nc, f"ost{i}") for i in range(3)]
    xnbB = A.alloc([D], BF16)
    xnTc = A.alloc([8, 128], BF16)
    rtA = A.alloc([8, 32], F32)
    rtB = A.alloc([8, 32], F32)
    qrot = A.alloc([512], BF16)
    krot = A.alloc([512], BF16)
    kdec = A.alloc([512], BF16)
    qkT = A.alloc([8, 128], BF16)
    qdT = A.alloc([4, 128], BF16)
    sTm = A.alloc([8, 128], BF16)
    vbf = A.alloc([D], BF16)
    tAu = A.alloc([D], F32)
    tBn = A.alloc([512], F32)
    retg = A.alloc([D], BF16)
    retgT = A.alloc([8, 128], BF16)
    tG = A.alloc([2048], F32)
    merged = A.alloc([D], BF16)
    mT = A.alloc([8, 128], BF16)
    attc_bufs = [A.alloc([4, 128], BF16) for _ in range(2)]
    attc_ring = Ring(attc_bufs)
    attld = [DSem(nc, f"attld{i}") for i in range(2)]
    junkb = A.alloc([128], BF16)

    wsem = DSem(nc, "wsem")
    csem3 = DSem(nc, "csem3")
    nchB = dbg.get('nchB', NCH) if dbg else NCH
    xload = {}

    def issue_xload(c):
        i = xc_ring.acquire(SP)
        dep = xcld[i].fire(nc.sync.dma_start(out=xc_bufs[i], in_=x[c * 128:(c + 1) * 128, :]))
        xload[c] = (i, dep)

    attload = {}

    def issue_attload(c):
        a = attc_ring.acquire(SP)
        dep2 = attld[a].fire(nc.sync.dma_start(out=attc_bufs[a], in_=attn_scr[c, :, :, :]))
        attload[c] = (a, dep2)

    sgi = xc_ring.acquire(SP)
    if nchB > 0:
        issue_xload(0)

    assert len(wq) == 8
    d_wB = wq[6:8]
    d_wap = [wdma(wap, attn_proj.rearrange("(k p) c -> p k c", p=128)), wq[-2]]
    d_bg = [wdma(bg_bf[0:1, :], b_gate[None, :]), wq[-2]]
    d_wout = [wdma(wout_sb, w_out.rearrange("(k p) c -> p k c", p=128)), wq[-2]]
    for src_, dst_ in ((c_cos, cosT), (c_sin, sinT), (c_decay, decay), (c_qd, qd), (c_kd, kd), (c_gc, gcol)):
        csem3.fire(nc.sync.dma_start(out=dst_, in_=src_))
    csem3.fire(nc.sync.dma_start(out=gnw_col, in_=ret_gn_w.rearrange("(k p) -> p k", p=128),
                                 allow_slow_non_contiguous=True))
    d_c3 = csem3.fire(nc.sync.dma_start(out=lnf_bc, in_=lnf_w.partition_broadcast(128)))
    rp_v = ret_proj.rearrange("(k p) c -> p k c", p=128)
    nc.vector.memset(R2, 0.0)
    nc.vector.memset(R_bf, 0.0)
    nc.vector.memset(ssB, 0.0)
    nc.vector.memset(ssF, 0.0)
    nc.vector.memset(ssG, 0.0)
    t_initB = DVE.mark(nc.vector.memset(nhalf8, -0.5))
    DVE.wait(d_c3)
    POOL.wait(t_initB)

    def rstd_chain(ss_ap, ms_ap, rs_ap, t_in, inv_n, neg_ap):
        DVE.wait(t_in)
        t_ms = DVE.mark(nc.vector.tensor_scalar(out=ms_ap, in0=ss_ap, scalar1=inv_n, scalar2=EPS,
                                                op0=ALU.mult, op1=ALU.add))
        POOL.wait(t_ms)
        return POOL.mark(nc.gpsimd.tensor_tensor(out=rs_ap, in0=ms_ap, in1=neg_ap, op=ALU.pow))

    free = {}

    def fr(name):
        return free.get(name)

    st = {}

    def P_norm(c):
        xi, d_x = xload.pop(c)
        s_ = st[c] = dict(xi=xi, xc=xc_bufs[xi])
        xc = s_["xc"]
        ACT.wait(d_x, fr("xnbB"))
        t_ss = ACT.mark(nc.scalar.activation(out=xnbB, in_=xc, func=AF.Square, accum_out=ssB[:, c:c + 1]))
        t_rs = rstd_chain(ssB[:, c:c + 1], msB[:, c:c + 1], rstdB[:, c:c + 1], t_ss, 1.0 / D, nhalf8[:, 0:1])
        DVE.wait(t_rs)
        s_["t_xn"] = DVE.mark(nc.vector.scalar_tensor_tensor(out=xnbB, in0=xc, scalar=rstdB[:, c:c + 1],
                                                             in1=ln1_bc, op0=ALU.mult, op1=ALU.mult))

    def P_T1(c):
        s_ = st[c]
        b0 = psum.acquire(PE, True)
        PE.wait(s_["t_xn"])
        for k in range(8):
            tr = nc.tensor.transpose(out=bk_bf(b0)[:, k, :], in_=xnbB[:, k * 128:(k + 1) * 128], identity=ident_bf)
        t_T1 = PE.mark(tr)
        free["xnbB"] = t_T1
        ACT.wait(t_T1, fr("xnTc"))
        s_["t_xT"] = ACT.mark(nc.scalar.copy(out=xnTc, in_=bk_bf(b0)))
        psum.release(b0, s_["t_xT"])

    def proj(c, col0):
        b = psum.acquire(PE, True)
        PE.wait(st[c]["t_xT"], d_wB)
        for k in range(8):
            mm = nc.tensor.matmul(bk(b), lhsT=xnTc[:, k, :], rhs=wB[:, k, col0:col0 + 512],
                                  start=(k == 0), stop=(k == 7))
        return b, mm

    def rope(c, b, dst, t_in, extra=None):
        cosb = cosT[:, c, :].unsqueeze(1).to_broadcast([128, 8, 32])
        sinb = sinT[:, c, :].unsqueeze(1).to_broadcast([128, 8, 32])
        src = bk(b).rearrange("p (h i two) -> p h i two", h=8, two=2)
        dv = dst.rearrange("p (h i two) -> p h i two", h=8, two=2)
        t1 = src[:, :, :, 0]
        t2 = src[:, :, :, 1]
        DVE.wait(t_in, extra)
        nc.vector.tensor_tensor(out=rtA, in0=t1, in1=cosb, op=ALU.mult)
        nc.vector.tensor_tensor(out=rtB, in0=t2, in1=sinb, op=ALU.mult)
        nc.vector.tensor_tensor(out=dv[:, :, :, 0], in0=rtA, in1=rtB, op=ALU.subtract)
        nc.vector.tensor_tensor(out=rtA, in0=t1, in1=sinb, op=ALU.mult)
        nc.vector.tensor_tensor(out=rtB, in0=t2, in1=cosb, op=ALU.mult)
        return DVE.mark(nc.vector.tensor_tensor(out=dv[:, :, :, 1], in0=rtA, in1=rtB, op=ALU.add))

    def P_projqk_pe(c):
        s_ = st[c]
        bq, mm = proj(c, 0)
        t_pq = PE.mark(mm)
        bkk, mm = proj(c, 512)
        t_pk = PE.mark(mm)
        s_["pqk"] = (bq, t_pq, bkk, t_pk)

    def P_rope(c):
        s_ = st[c]
        bq, t_pq, bkk, t_pk = s_["pqk"]
        s_["t_rq"] = rope(c, bq, qrot, t_pq, fr("qkrot"))
        psum.release(bq, s_["t_rq"])
        s_["t_rk"] = rope(c, bkk, krot, t_pk)
        psum.release(bkk, s_["t_rk"])
        DVE.wait(fr("kdec"))
        s_["t_kd"] = DVE.mark(nc.vector.tensor_tensor(
            out=kdec.rearrange("p (h e) -> p h e", h=8), in0=krot.rearrange("p (h e) -> p h e", h=8),
            in1=kd[:, :].unsqueeze(2).to_broadcast([128, 8, 64]), op=ALU.mult))

    def P_projv(c):
        s_ = st[c]
        bv0, mm = proj(c, 1024)
        bv1, mm = proj(c, 1536)
        t_pv = PE.mark(mm)
        ACT.wait(t_pv, fr("vbf"))
        nc.scalar.copy(out=vbf[:, 0:512], in_=bk(bv0))
        s_["t_v"] = ACT.mark(nc.scalar.copy(out=vbf[:, 512:1024], in_=bk(bv1)))
        psum.release(bv0, s_["t_v"])
        psum.release(bv1, s_["t_v"])

    def P_projrg(c):
        s_ = st[c]
        brg0, mm = proj(c, 2048)
        brg1, mm = proj(c, 2560)
        t_prg = PE.mark(mm)
        ACT.wait(t_prg, fr("tAu"))
        nc.scalar.activation(out=tAu[:, 0:512], in_=bk(brg0), func=AF.Tanh, scale=0.5)
        t_tg = ACT.mark(nc.scalar.activation(out=tAu[:, 512:1024], in_=bk(brg1), func=AF.Tanh, scale=0.5))
        DVE.wait(t_tg)
        nc.vector.scalar_tensor_tensor(out=tAu[:, 0:512], in0=tAu[:, 0:512], scalar=1.0, in1=bk(brg0),
                                       op0=ALU.add, op1=ALU.mult)
        s_["t_u"] = DVE.mark(nc.vector.scalar_tensor_tensor(out=tAu[:, 512:1024], in0=tAu[:, 512:1024], scalar=1.0,
                                                            in1=bk(brg1), op0=ALU.add, op1=ALU.mult))
        psum.release(brg0, s_["t_u"])
        psum.release(brg1, s_["t_u"])

    def P_T2(c):
        s_ = st[c]
        bT2 = psum.acquire(PE, True)
        PE.wait(s_["t_rq"], s_["t_rk"])
        for j in range(4):
            nc.tensor.transpose(out=bk_bf(bT2)[:, j, :], in_=qrot[:, j * 128:(j + 1) * 128], identity=ident_bf)
        for j in range(4):
            tr = nc.tensor.transpose(out=bk_bf(bT2)[:, 4 + j, :], in_=krot[:, j * 128:(j + 1) * 128],
                                     identity=ident_bf)
        t_T2 = PE.mark(tr)
        free["qkrot"] = t_T2
        ACT.wait(t_T2, fr("qkT"))
        s_["t_qk"] = ACT.mark(nc.scalar.copy(out=qkT, in_=bk_bf(bT2)))
        DVE.wait(t_T2, fr("qdT"))
        s_["t_qd"] = DVE.mark(nc.vector.tensor_tensor(out=qdT, in0=bk_bf(bT2)[:, 0:4, :], in1=qd, op=ALU.mult))
        psum.release(bT2, s_["t_qk"], s_["t_qd"])

    def P_S(c):
        s_ = st[c]
        for pb in range(2):
            b = psum.acquire(PE, True)
            PE.wait(s_["t_qk"])
            po = 64 * pb
            for j in range(4):
                mm = nc.tensor.matmul(bk4(b)[:, j, :], lhsT=qkT[po:po + 64, 4 + j, :], rhs=qkT[po:po + 64, j, :],
                                      start=True, stop=True)
            t_S = PE.mark(mm)
            DVE.wait(t_S, fr("sTm"))
            s_["t_sm"] = DVE.mark(nc.vector.tensor_tensor(out=sTm[:, pb::2, :], in0=bk4(b),
                                                          in1=decay[:, pb::2, :], op=ALU.mult))
            psum.release(b, s_["t_sm"])

    def P_o(c):
        s_ = st[c]
        bo = []
        for pb in range(2):
            b = psum.acquire(PE, True)
            PE.wait(s_["t_sm"], s_["t_v"], s_["t_qd"], fr("R_bf_ready"))
            po = 64 * pb
            for j in range(4):
                h = 2 * j + pb
                nc.tensor.matmul(bk4(b)[:, j, :], lhsT=sTm[:, h, :], rhs=vbf[:, h * 128:(h + 1) * 128],
                                 start=True, stop=False)
                mm = nc.tensor.matmul(bk4(b)[:, j, :], lhsT=qdT[po:po + 64, j, :], rhs=R_bf[po:po + 64, j, :],
                                      start=False, stop=True)
            t_o = PE.mark(mm)
            bo.append(b)
        s_["bo"] = bo
        s_["t_o"] = t_o
        free["sTm"] = t_o
        free["qdT"] = t_o
        free["qkT"] = t_o

    def P_kv(c):
        s_ = st[c]
        bkv = []
        for jb in range(2):
            b = psum.acquire(PE, True)
            PE.wait(s_["t_kd"], s_["t_v"])
            for jj in range(2):
                j = 2 * jb + jj
                mm = nc.tensor.matmul(bk(b)[:, jj * 256:(jj + 1) * 256], lhsT=kdec[:, j * 128:(j + 1) * 128],
                                      rhs=vbf[:, j * 256:(j + 1) * 256], start=True, stop=True)
            bkv.append(b)
        t_kv = PE.mark(mm)
        free["kdec"] = t_kv
        free["vbf"] = t_kv
        DVE.wait(t_kv, s_["t_o"])
        nc.vector.tensor_tensor(out=R2, in0=R2, in1=gcol[:, :].unsqueeze(2).to_broadcast([128, 4, 128]),
                                op=ALU.mult)
        for jb in range(2):
            kvv = bk(bkv[jb]).rearrange("p (j e) -> p j e", j=2)
            nc.vector.tensor_tensor(out=R2[0:64, 2 * jb:2 * jb + 2, :], in0=kvv[0:64, :, 0:128],
                                    in1=R2[0:64, 2 * jb:2 * jb + 2, :], op=ALU.add)
            t_R = DVE.mark(nc.vector.tensor_tensor(out=R2[64:128, 2 * jb:2 * jb + 2, :], in0=kvv[64:128, :, 128:256],
                                                   in1=R2[64:128, 2 * jb:2 * jb + 2, :], op=ALU.add))
            psum.release(bkv[jb], t_R)
        free["R_bf_ready"] = DVE.mark(nc.vector.tensor_copy(out=R_bf, in_=R2))

    def P_mg(c, jqs):
        s_ = st[c]
        for jq in jqs:
            b = psum.acquire(PE, True)
            PE.wait(s_["t_xT"], d_wB, d_bg)
            for k in range(8):
                nc.tensor.matmul(bk(b), lhsT=xnTc[:, k, :], rhs=wB[:, k, 3072 + jq * 512:3072 + (jq + 1) * 512],
                                 start=(k == 0), stop=False)
            mm = nc.tensor.matmul(bk(b), lhsT=ones_bf[0:1, :], rhs=bg_bf[0:1, jq * 512:(jq + 1) * 512],
                                  start=False, stop=True)
            t_mg = PE.mark(mm)
            if jq == 3:
                free["xnTc"] = t_mg
            if jq == 0:
                ACT.wait(fr("tG"))
            ACT.wait(t_mg)
            s_["t_tG"] = ACT.mark(nc.scalar.activation(out=tG[:, jq * 512:(jq + 1) * 512], in_=bk(b),
                                                       func=AF.Tanh, scale=0.5))
            psum.release(b, s_["t_tG"])

    def P_gn(c):
        s_ = st[c]
        bo = s_["bo"]
        ACT.wait(s_["t_o"])
        for h in range(8):
            t_sq = ACT.mark(nc.scalar.activation(out=junkb, in_=bk4(bo[h % 2])[:, h // 2, :], func=AF.Square,
                                                 accum_out=ssG[:, c * 8 + h:c * 8 + h + 1]))
        t_rg = rstd_chain(ssG[:, c * 8:c * 8 + 8], msG[:, c * 8:c * 8 + 8], rsG[:, c * 8:c * 8 + 8], t_sq,
                          1.0 / 128, nhalf8)
        DVE.wait(t_rg, s_["t_u"], fr("retg"))
        retg_v = retg.rearrange("p (h e) -> p h e", h=8)
        tAu_v = tAu.rearrange("p (h e) -> p h e", h=8)
        tBn_v = tBn.rearrange("p (h e) -> p h e", h=4)
        for pb in range(2):
            nc.vector.tensor_tensor(out=tBn_v, in0=bk4(bo[pb]),
                                    in1=rsG[:, c * 8 + pb:c * 8 + 8:2].unsqueeze(2).to_broadcast([128, 4, 128]),
                                    op=ALU.mult)
            t_ret = DVE.mark(nc.vector.scalar_tensor_tensor(out=retg_v[:, pb::2, :], in0=tBn_v, scalar=0.5,
                                                            in1=tAu_v[:, pb::2, :],
                                                            op0=ALU.mult, op1=ALU.mult))
            psum.release(bo[pb], t_ret, t_sq)
        s_["t_ret"] = t_ret
        free["tAu"] = t_ret

    def P_T3(c):
        s_ = st[c]
        bT3 = psum.acquire(PE, True)
        PE.wait(s_["t_ret"])
        for k in range(8):
            tr = nc.tensor.transpose(out=bk_bf(bT3)[:, k, :], in_=retg[:, k * 128:(k + 1) * 128], identity=ident_bf)
        t_T3 = PE.mark(tr)
        free["retg"] = t_T3
        ACT.wait(t_T3, fr("retgT"))
        s_["t_rT"] = ACT.mark(nc.scalar.copy(out=retgT, in_=bk_bf(bT3)))
        psum.release(bT3, s_["t_rT"])

    def P_yattn(c):
        s_ = st[c]
        ai, d_att = attload.pop(c)
        bya = []
        for jj in range(2):
            b = psum.acquire(PE, True)
            PE.wait(d_att, d_wap)
            for k in range(4):
                mm = nc.tensor.matmul(bk(b), lhsT=attc_bufs[ai][:, k, :], rhs=wap[:, k, jj * 512:(jj + 1) * 512],
                                      start=(k == 0), stop=(k == 3))
            bya.append(b)
        s_["t_ya"] = PE.mark(mm)
        s_["bya"] = bya
        attc_ring.release(ai, s_["t_ya"])
        if c + 2 < nchB:
            issue_attload(c + 2)

    def P_yret(c):
        s_ = st[c]
        byr = []
        for jj in range(2):
            b = psum.acquire(PE, True)
            PE.wait(s_["t_rT"], wrp_state["t"])
            for k in range(8):
                mm = nc.tensor.matmul(bk(b), lhsT=retgT[:, k, :], rhs=wrp[:, k, jj * 512:(jj + 1) * 512],
                                      start=(k == 0), stop=(k == 7))
            byr.append(b)
        s_["t_yr"] = PE.mark(mm)
        s_["byr"] = byr
        free["retgT"] = s_["t_yr"]

    def P_merge(c):
        s_ = st[c]
        bya, byr = s_["bya"], s_["byr"]
        DVE.wait(s_["t_ya"], s_["t_yr"], s_["t_tG"], fr("merged"))
        for jj in range(2):
            s0 = slice(jj * 512, (jj + 1) * 512)
            s1 = slice(1024 + jj * 512, 1024 + (jj + 1) * 512)
            nc.vector.scalar_tensor_tensor(out=tG[:, s0], in0=tG[:, s0], scalar=1.0, in1=bk(bya[jj]),
                                           op0=ALU.add, op1=ALU.mult)
            nc.vector.scalar_tensor_tensor(out=tG[:, s1], in0=tG[:, s1], scalar=1.0, in1=bk(byr[jj]),
                                           op0=ALU.add, op1=ALU.mult)
            t_mg2 = DVE.mark(nc.vector.tensor_tensor(out=merged[:, s0], in0=tG[:, s0], in1=tG[:, s1], op=ALU.add))
            psum.release(bya[jj], t_mg2)
            psum.release(byr[jj], t_mg2)
        s_["t_mg2"] = t_mg2
        free["tG"] = t_mg2

    def P_T4(c):
        s_ = st[c]
        bT4 = psum.acquire(PE, True)
        PE.wait(s_["t_mg2"])
        for k in range(8):
            tr = nc.tensor.transpose(out=bk_bf(bT4)[:, k, :], in_=merged[:, k * 128:(k + 1) * 128], identity=ident_bf)
        t_T4 = PE.mark(tr)
        s_["t_T4"] = t_T4
        ACT.wait(t_T4, fr("mT"))
        s_["t_mT"] = ACT.mark(nc.scalar.copy(out=mT, in_=bk_bf(bT4)))
        psum.release(bT4, s_["t_mT"])

    def P_out(c):
        s_ = st[c]
        xc = s_["xc"]
        xi = s_["xi"]
        for jj in range(2):
            b = psum.acquire(PE, True)
            PE.wait(s_["t_mT"], d_wout)
            for k in range(8):
                mm = nc.tensor.matmul(bk(b), lhsT=mT[:, k, :], rhs=wout_sb[:, k, jj * 512:(jj + 1) * 512],
                                      start=(k == 0), stop=(k == 7))
            t_po = PE.mark(mm)
            DVE.wait(t_po)
            sl_ = slice(jj * 512, (jj + 1) * 512)
            t_h = DVE.mark(nc.vector.scalar_tensor_tensor(out=xc[:, sl_], in0=bk(b), scalar=0.5, in1=xc[:, sl_],
                                                          op0=ALU.mult, op1=ALU.add))
            psum.release(b, t_h)
        free["mT"] = t_po
        ACT.wait(t_h)
        t_ssf = ACT.mark(nc.scalar.activation(out=merged, in_=xc, func=AF.Square, accum_out=ssF[:, c:c + 1]))
        free["merged"] = [s_["t_T4"], t_ssf]
        t_rsf = rstd_chain(ssF[:, c:c + 1], msF[:, c:c + 1], rstdF[:, c:c + 1], t_ssf, 1.0 / D, nhalf8[:, 0:1])
        DVE.wait(t_rsf)
        t_out = DVE.mark(nc.vector.scalar_tensor_tensor(out=xc, in0=xc, scalar=rstdF[:, c:c + 1], in1=lnf_bc,
                                                        op0=ALU.mult, op1=ALU.mult))
        SP.wait(t_out)
        d_o = ost[xi].fire(nc.sync.dma_start(out=out[c * 128:(c + 1) * 128, :], in_=xc))
        xc_ring.release(xi, d_o)
        del st[c]
        return d_o

    NPF = 2
    for c in range(min(NPF, nchB)):
        if c > 0:
            issue_xload(c)
        issue_attload(c)
    stg_sem = DSem(nc, "stg")
    wrp_state = {"t": None}

    def mk_wrp(k):
        def f():
            SP.wait(wrp_state["t"])
            dd = stg_sem.fire(nc.sync.dma_start(out=xc_bufs[sgi], in_=rp_v[:, k, :]))
            DVE.wait(dd, d_c3)
            wrp_state["t"] = DVE.mark(nc.vector.tensor_scalar(out=wrp[:, k, :], in0=xc_bufs[sgi],
                                                              scalar1=gnw_col[:, k:k + 1], scalar2=None,
                                                              op0=ALU.mult))
            if k == 7:
                xc_ring.release(sgi, wrp_state["t"])
        return f

    wrp_steps = [mk_wrp(k) for k in range(8)]

    def wrp_step():
        if wrp_steps:
            wrp_steps.pop(0)()
    d_last = []
    if nchB > 0:
        P_norm(0)
        wrp_step()
        P_T1(0)
        wrp_step()
        P_projqk_pe(0)
        P_rope(0)
        wrp_step()
        P_projv(0)
        wrp_step()
        P_projrg(0)
        wrp_step()
    for c in range(nchB):
        nxt = c + 1 < nchB
        if nxt:
            P_norm(c + 1)
        P_T2(c)
        wrp_step()
        P_yattn(c)
        wrp_step()
        P_S(c)
        wrp_step()
        if c + NPF < nchB:
            issue_xload(c + NPF)
        P_mg(c, [0])
        P_o(c)
        P_kv(c)
        P_gn(c)
        P_mg(c, [1, 2, 3])
        P_T3(c)
        if nxt:
            P_T1(c + 1)
        P_yret(c)
        if nxt:
            P_projqk_pe(c + 1)
        P_merge(c)
        if nxt:
            P_rope(c + 1)
        P_T4(c)
        if nxt:
            P_projv(c + 1)
        d_last.append(P_out(c))
        if nxt:
            P_projrg(c + 1)
    SP.wait(*[d_ for d_ in d_last[-3:] if d_ is not None])
    SP.wait((PE, PE.n), (ACT, ACT.n), (DVE, DVE.n), (POOL, POOL.n))
    return nc


def kernel(x, ln1_w, w_in, b_gate, attn_proj, ret_proj, ret_gn_w, w_out, lnf_w, _stage=2, _cores=8, _dbg=None):
    nc = build_nc(stage=_stage, dbg=_dbg)
    consts = host_consts()
    x = np.ascontiguousarray(x, dtype=np.float32)
    shared = {
        "ln1_w": np.ascontiguousarray(ln1_w[0], dtype=np.float32),
        "w_in": np.ascontiguousarray(w_in[0], dtype=np.float32),
        "b_gate": np.ascontiguousarray(b_gate[0], dtype=np.float32),
        "attn_proj": np.ascontiguousarray(attn_proj[0], dtype=np.float32),
        "ret_proj": np.ascontiguousarray(ret_proj[0], dtype=np.float32),
        "ret_gn_w": np.ascontiguousarray(ret_gn_w[0], dtype=np.float32),
        "w_out": np.ascontiguousarray(w_out[0], dtype=np.float32),
        "lnf_w": np.ascontiguousarray(lnf_w, dtype=np.float32),
    }
    shared.update(consts)
    in_maps = [dict(shared, x=x[i]) for i in range(_cores)]
    res = run_bass_kernel_spmd(nc, in_maps, core_ids=list(range(_cores)))
    if _dbg is not None:
        return res.results
    if _stage == 0:
        return [r["dbg_xnT"] for r in res.results]
    if _stage == 1:
        return [r["attn_scr"] for r in res.results]
    return np.stack([r["out"] for r in res.results], axis=0)
```
